# Optimizing a Trainium2 kernel written in Bass

```python
import jax, jax.numpy as jnp
from jax import lax
import numpy as np

D_MODEL = 2048
BATCH = 8
SEQ = 2048
DEPTH = 1

MIX_WIDTH = D_MODEL
POOL_WIDTH = MIX_WIDTH // 2
N_POOL_GROUPS = 4
POOL_GROUP_DIM = POOL_WIDTH // N_POOL_GROUPS
POOL_WINDOWS = (2, 4, 8, 16)
RET_WIDTH = MIX_WIDTH - POOL_WIDTH
RET_HEADS = 8
RET_HEAD_DIM = RET_WIDTH // RET_HEADS
RET_CHUNK = 128
ROPE_BASE = 10000.0
IN_COLS = POOL_WIDTH + 4 * RET_WIDTH
D_FF = 5504
CONV_WIDTH = 3
PLE_DIM = 256
NORM_EPS = 1e-6
GN_EPS = 1e-5

kernel_name = "hybrid_pool_retention_convffn_ple"


def rmsnorm(x, w):
    xf = x.astype(jnp.float32)
    y = xf * lax.rsqrt(jnp.mean(xf * xf, axis=-1, keepdims=True) + NORM_EPS)
    return (y * w.astype(jnp.float32)).astype(x.dtype)


def pool_mixer(u, pool_w, pool_scale):
    b, s, _ = u.shape
    ug = u.reshape(b, s, N_POOL_GROUPS, POOL_GROUP_DIM)
    cs = jnp.cumsum(ug.astype(jnp.float32), axis=1)
    cs = jnp.pad(cs, ((0, 0), (1, 0), (0, 0), (0, 0)))
    t = jnp.arange(s)
    means = []
    for g, w in enumerate(POOL_WINDOWS):
        lo = jnp.maximum(t + 1 - w, 0)
        cnt = (t + 1 - lo).astype(jnp.float32)
        win = cs[:, 1:, g] - cs[:, lo, g]
        means.append(win / cnt[None, :, None])
    mean = jnp.stack(means, axis=2).astype(u.dtype)
    y = jnp.einsum('bsgc,gcd->bsgd', mean - ug, pool_w)
    return y.reshape(b, s, POOL_WIDTH) * pool_scale


def rotary(x, cos, sin):
    x1, x2 = jnp.split(x, 2, axis=-1)
    c = cos[None, :, None, :]
    s_ = sin[None, :, None, :]
    return jnp.concatenate([x1 * c - x2 * s_, x2 * c + x1 * s_], axis=-1)


def retention_chunkwise(q, k, v):
    b, s, h, dk = q.shape
    dv = v.shape[-1]
    n = s // RET_CHUNK
    C = RET_CHUNK
    qc = q.reshape(b, n, C, h, dk)
    kc = k.reshape(b, n, C, h, dk)
    vc = v.reshape(b, n, C, h, dv)
    gamma = 1.0 - jnp.exp2(-5.0 - jnp.arange(h, dtype=jnp.float32))
    log_g = jnp.log(gamma)
    idx = jnp.arange(C, dtype=jnp.float32)
    diff = idx[:, None] - idx[None, :]
    decay = jnp.where(diff[None] >= 0,
                      jnp.exp(jnp.maximum(diff, 0.0)[None] * log_g[:, None, None]),
                      0.0)
    scores = jnp.einsum('bnihd,bnjhd->bnhij', qc, kc) * decay
    intra = jnp.einsum('bnhij,bnjhe->bnihe', scores, vc)
    zeta = jnp.exp((C - 1.0 - idx)[None, :] * log_g[:, None])
    kv = jnp.einsum('bnjhd,hj,bnjhe->nbhde', kc, zeta, vc)
    g_chunk = jnp.exp(C * log_g)[:, None, None]

    def step(state, kv_i):
        return g_chunk * state + kv_i, state

    init = jnp.zeros((b, h, dk, dv), dtype=kv.dtype)
    _, r_prev = lax.scan(step, init, kv)
    xi = jnp.exp((idx + 1.0)[None, :] * log_g[:, None])
    cross = jnp.einsum('bnihd,nbhde,hi->bnihe', qc, r_prev, xi)
    return (intra + cross).reshape(b, s, h, dv).astype(q.dtype)


def head_groupnorm(y, w):
    yf = y.astype(jnp.float32)
    mu = jnp.mean(yf, axis=-1, keepdims=True)
    var = jnp.mean(jnp.square(yf - mu), axis=-1, keepdims=True)
    return ((yf - mu) * lax.rsqrt(var + GN_EPS) * w.astype(jnp.float32)).astype(y.dtype)


def causal_dwconv(x, w, bias):
    s = x.shape[1]
    xp = jnp.pad(x, ((0, 0), (CONV_WIDTH - 1, 0), (0, 0)))
    out = bias
    for j in range(CONV_WIDTH):
        out = out + xp[:, j:j + s] * w[j]
    return out


def setup_inputs(seed: int = 0) -> dict:
    key = jax.random.key(seed)
    ks = jax.random.split(key, 20)
    f32 = jnp.float32
    nrm = lambda k, shape, scale: jax.random.normal(k, shape, f32) * scale
    return {
        "x": nrm(ks[0], (BATCH, SEQ, D_MODEL), 1.0),
        "p": nrm(ks[1], (DEPTH, BATCH, SEQ, PLE_DIM), 1.0),
        "norm1_w": 1.0 + nrm(ks[2], (DEPTH, D_MODEL), 0.05),
        "w_in": nrm(ks[3], (DEPTH, D_MODEL, IN_COLS), D_MODEL ** -0.5),
        "pool_w": nrm(ks[4], (DEPTH, N_POOL_GROUPS, POOL_GROUP_DIM, POOL_GROUP_DIM), POOL_GROUP_DIM ** -0.5),
        "pool_scale": 1.0 + nrm(ks[5], (DEPTH, POOL_WIDTH), 0.1),
        "ret_gn_w": 1.0 + nrm(ks[6], (DEPTH, RET_WIDTH), 0.05),
        "w_out": nrm(ks[7], (DEPTH, MIX_WIDTH, D_MODEL), MIX_WIDTH ** -0.5),
        "norm2_w": 1.0 + nrm(ks[8], (DEPTH, D_MODEL), 0.05),
        "w_up": nrm(ks[9], (DEPTH, D_MODEL, 2 * D_FF), D_MODEL ** -0.5),
        "conv_w": nrm(ks[10], (DEPTH, CONV_WIDTH, 2 * D_FF), CONV_WIDTH ** -0.5),
        "conv_b": nrm(ks[11], (DEPTH, 2 * D_FF), 0.02),
        "w_down": nrm(ks[12], (DEPTH, D_FF, D_MODEL), D_FF ** -0.5),
        "norm3_w": 1.0 + nrm(ks[13], (DEPTH, D_MODEL), 0.05),
        "ple_gate_w": nrm(ks[14], (DEPTH, D_MODEL, D_MODEL), D_MODEL ** -0.5),
        "ple_proj_w": nrm(ks[15], (DEPTH, PLE_DIM, D_MODEL), PLE_DIM ** -0.5),
        "final_norm_w": 1.0 + nrm(ks[16], (D_MODEL,), 0.05),
    }


def reference(x, p, norm1_w, w_in, pool_w, pool_scale, ret_gn_w, w_out, norm2_w,
              w_up, conv_w, conv_b, w_down, norm3_w, ple_gate_w, ple_proj_w, final_norm_w):
    b, s, _ = x.shape
    pos = jnp.arange(s, dtype=jnp.float32)
    inv_freq = 1.0 / (ROPE_BASE ** (jnp.arange(0, RET_HEAD_DIM, 2, dtype=jnp.float32) / RET_HEAD_DIM))
    ang = pos[:, None] * inv_freq[None, :]
    cos = jnp.cos(ang).astype(x.dtype)
    sin = jnp.sin(ang).astype(x.dtype)
    k_scale = RET_HEAD_DIM ** -0.5
    splits = [POOL_WIDTH, POOL_WIDTH + RET_WIDTH, POOL_WIDTH + 2 * RET_WIDTH, POOL_WIDTH + 3 * RET_WIDTH]

    h = x
    for i in range(DEPTH):
        a = rmsnorm(h, norm1_w[i])
        z = a @ w_in[i]
        u, q, k, v, g = jnp.split(z, splits, axis=-1)
        pool_out = pool_mixer(u, pool_w[i], pool_scale[i])
        q = rotary(q.reshape(b, s, RET_HEADS, RET_HEAD_DIM), cos, sin)
        k = rotary(k.reshape(b, s, RET_HEADS, RET_HEAD_DIM), cos, sin) * k_scale
        v = v.reshape(b, s, RET_HEADS, RET_HEAD_DIM)
        r = retention_chunkwise(q, k, v)
        r = head_groupnorm(r, ret_gn_w[i].reshape(RET_HEADS, RET_HEAD_DIM)).reshape(b, s, RET_WIDTH)
        r = jax.nn.silu(g) * r
        h = h + jnp.concatenate([pool_out, r], axis=-1) @ w_out[i]
        c = rmsnorm(h, norm2_w[i])
        up = causal_dwconv(c @ w_up[i], conv_w[i], conv_b[i])
        gate, val = jnp.split(up, 2, axis=-1)
        h = h + (jax.nn.silu(gate) * val) @ w_down[i]
        e = rmsnorm(h, norm3_w[i])
        h = h + jax.nn.sigmoid(e @ ple_gate_w[i]) * (p[i] @ ple_proj_w[i])
    return rmsnorm(h, final_norm_w)
```

```python
import numpy as np
import ml_dtypes
from contextlib import ExitStack
import concourse.bass as bass
import concourse.mybir as mybir
from concourse.bass_utils import run_bass_kernel_spmd

F32 = mybir.dt.float32
BF16 = mybir.dt.bfloat16
AF = mybir.ActivationFunctionType
ALU = mybir.AluOpType
AX = mybir.AxisListType

D = 2048
S = 2048
TT = 512
NSUB = 4
KC = 16
DFF = 5504
NFC = 43
H = 8
HD = 128
PLE = 256
INC = 5120
NSLOT = 3
POOL_WINDOWS = (2, 4, 8, 16)
NORM_EPS = 1e-6
GN_EPS = 1e-5
GRAN = 64


def granules(ap):
    name = ap.tensor.name
    if name.startswith("ps"):
        return {(name, -1)}
    dims = ap.ap
    esz = mybir.dt.size(ap.dtype)
    rowlen = dims[0][0]
    col0 = (ap.offset % rowlen) if rowlen > 0 else ap.offset
    free = [(s, c) for (s, c) in dims[1:] if c > 1 and s != 0]
    if not free:
        runs = [(col0, 1)]
    else:
        free.sort(key=lambda sc: -abs(sc[0]))
        inner_s, inner_c = free[-1]
        if inner_s == 1:
            outer = free[:-1]
            runlen = inner_c
        else:
            outer = free
            runlen = 1
        nrun = 1
        for s, c in outer:
            nrun *= c
        if nrun > 128:
            lo = col0
            hi = col0 + sum(s * (c - 1) for s, c in free) + 1
            runs = [(lo, hi - lo)]
        else:
            starts = [col0]
            for s, c in outer:
                starts = [b + s * i for b in starts for i in range(c)]
            runs = [(b, runlen) for b in starts]
    keys = set()
    for b, n in runs:
        g0 = (b * esz) // GRAN
        g1 = ((b + n) * esz - 1) // GRAN
        for g in range(g0, g1 + 1):
            keys.add((name, g))
    return keys


class Op:
    __slots__ = ("idx", "eng", "fn", "rk", "wk", "dkey", "dtick", "deps", "sig", "ticket", "waits")

    def __init__(self):
        self.deps = {}
        self.sig = False
        self.ticket = 0
        self.waits = []
        self.dkey = None
        self.dtick = 0


class Prog:
    ENGS = ("pe", "act", "dve", "pool", "sp")

    def __init__(self):
        self.ops = []
        self.dcount = {}
        self.wait_all_keys = set()

    def add(self, eng, fn, reads=(), writes=(), dkey=None, wait_all=False):
        op = Op()
        op.idx = len(self.ops)
        op.eng = eng
        op.fn = fn
        rk = set()
        for a in reads:
            rk |= granules(a)
        wk = set()
        for a in writes:
            wk |= granules(a)
        op.rk = rk
        op.wk = wk
        if dkey is not None:
            op.dkey = dkey
            self.dcount[dkey] = self.dcount.get(dkey, 0) + 1
            op.dtick = 16 * self.dcount[dkey]
            if wait_all:
                self.wait_all_keys.add(dkey)
        self.ops.append(op)
        return op

    def analyze(self):
        ops = self.ops
        state = {}
        for op in ops:
            deps = op.deps
            for k in op.rk:
                st = state.get(k)
                if st is not None and st[0] >= 0:
                    deps[st[0]] = True
                if st is not None and k[1] == -1:
                    for rk_, r in st[1].items():
                        if rk_ != op.eng and r not in deps:
                            deps[r] = False
            for k in op.wk:
                st = state.get(k)
                if st is not None:
                    if st[0] >= 0:
                        deps[st[0]] = True
                    for r in st[1].values():
                        if r not in deps:
                            deps[r] = False
            deps.pop(op.idx, None)
            rkey = op.eng if op.dkey is None else ("d", op.idx)
            for k in op.rk:
                st = state.get(k)
                if st is None:
                    state[k] = [-1, {rkey: op.idx}]
                else:
                    st[1][rkey] = op.idx
            for k in op.wk:
                state[k] = [op.idx, {}]
        need = []
        for op in ops:
            lst = []
            for d, strong in op.deps.items():
                Dp = ops[d]
                if Dp.dkey is not None:
                    lst.append(d)
                elif Dp.eng == op.eng:
                    if op.eng == "pe" and op.dkey is None:
                        continue
                    lst.append(d)
                    Dp.sig = True
                else:
                    lst.append(d)
                    Dp.sig = True
            need.append(lst)
        cnt = {e: 0 for e in self.ENGS}
        for op in ops:
            if op.dkey is None and op.sig:
                cnt[op.eng] += 1
                op.ticket = cnt[op.eng]
        self.sig_total = cnt
        waited = {e: {} for e in self.ENGS}
        for op, lst in zip(ops, need):
            req = {}
            for d in lst:
                Dp = ops[d]
                if Dp.dkey is not None:
                    sk = ("d", Dp.dkey)
                    val = Dp.dtick
                    if Dp.dkey in self.wait_all_keys:
                        val = 16 * self.dcount[Dp.dkey]
                else:
                    sk = ("e", Dp.eng)
                    val = Dp.ticket
                if val > req.get(sk, 0):
                    req[sk] = val
            w = waited[op.eng]
            for sk, val in req.items():
                if w.get(sk, 0) >= val:
                    continue
                w[sk] = val
                op.waits.append((sk, val))

    def emit(self, engname, eng, sems):
        for op in self.ops:
            if op.eng != engname:
                continue
            for sk, val in op.waits:
                eng.wait_ge(sems[sk], val)
            ins = op.fn(eng)
            if op.dkey is not None:
                ins.then_inc(sems[("d", op.dkey)], 16)
            elif op.sig:
                ins.then_inc(sems[("e", engname)], 1)


def make_consts(nt):
    c = {}
    c["ident"] = np.eye(128, dtype=np.float32).astype(ml_dtypes.bfloat16)
    pos = np.arange(S, dtype=np.float32)
    inv_freq = (1.0 / (np.float32(10000.0) ** (np.arange(0, HD, 2, dtype=np.float32) / np.float32(HD)))).astype(np.float32)
    ang = (pos[:, None] * inv_freq[None, :]).astype(np.float32)
    cos = np.cos(ang.astype(np.float64)).astype(np.float32)
    sin = np.sin(ang.astype(np.float64)).astype(np.float32)
    ks = np.float32(HD ** -0.5)
    tabs = [cos, sin, -sin, cos * ks, sin * ks, -sin * ks]
    rot = np.zeros((nt, 128, 6, NSUB, 64), np.float32)
    for a, tb in enumerate(tabs):
        rot[:, :, a] = tb[: nt * TT].reshape(nt, NSUB, 128, 64).transpose(0, 2, 1, 3)
    c["rot"] = rot.reshape(nt, 128, 6 * NSUB * 64)
    gamma = 1.0 - np.exp2(-5.0 - np.arange(H, dtype=np.float64))
    pidx = np.arange(128, dtype=np.float64)
    gt = np.zeros((128, 24), np.float64)
    gt[:, 0:8] = gamma[None, :] ** pidx[:, None]
    gt[:, 8:16] = gamma[None, :] ** (-pidx[:, None])
    gt[:, 16:24] = (gamma ** 128.0)[None, :]
    c["gtab"] = gt.astype(np.float32)
    j = np.arange(128)[:, None]
    i = np.arange(128)[None, :]
    c["maskT"] = (i >= j).astype(np.float32)
    bands = np.zeros((128, 12, 128), np.float32)
    tp = np.arange(128)[:, None]
    t = np.arange(128)[None, :]
    for g, w in enumerate(POOL_WINDOWS):
        cur = ((t - tp >= 0) & (t - tp <= w - 1)).astype(np.float32)
        cur_g = cur.copy()
        cur_g[np.arange(128), np.arange(128)] = 1.0 - w
        bands[:, g, :] = cur_g
        prev = (((t + 128) - tp) <= w - 1).astype(np.float32)
        bands[:, 4 + g, :] = prev
        cnt = np.minimum(np.arange(128) + 1, w).astype(np.float32)
        cur0 = cur.copy()
        cur0[np.arange(128), np.arange(128)] = 1.0 - cnt
        bands[:, 8 + g, :] = cur0
    c["bands"] = bands.reshape(128, 12 * 128).astype(ml_dtypes.bfloat16)
    inv0 = np.zeros((4, 128), np.float32)
    for g, w in enumerate(POOL_WINDOWS):
        inv0[g] = 1.0 / np.minimum(np.arange(128) + 1, w).astype(np.float32)
    c["invcnt0"] = np.ascontiguousarray(np.broadcast_to(inv0.reshape(1, 512), (128, 512))).astype(np.float32)
    return c


def cols(v, n):
    return np.ascontiguousarray(np.asarray(v, np.float32).reshape(n, 128).T)


def build_nc(nt=4, stop_after=None):
    nc = bass.Bass("TRN2", target_bir_lowering=False)
    ntok = nt * TT

    def din(name, shape, dt=F32):
        return nc.dram_tensor(name, list(shape), dt, kind="ExternalInput").ap()

    x_d = din("x", [ntok, D])
    p_d = din("p", [ntok, PLE])
    w_in_d = din("w_in", [D, INC])
    pool_w_d = din("pool_w", [4 * 256, 256])
    w_out_d = din("w_out", [D, D])
    w_up_d = din("w_up", [D, 2 * DFF])
    w_down_d = din("w_down", [DFF, D])
    wg_d = din("ple_gate_w", [D, D])
    wp_d = din("ple_proj_w", [PLE, D])
    nwc_d = din("nwc", [128, 48])
    plsc_d = din("plsc", [128, 8])
    gnwc_d = din("gnwc", [128, 8])
    cw_d = din("cw", [128, 86 * 3])
    cb_d = din("cb", [128, 86])
    wf_d = din("wf_tab", [128, D])
    ident_d = din("ident", [128, 128], BF16)
    rot_d = din("rot", [nt, 128, 6 * NSUB * 64])
    gtab_d = din("gtab", [128, 24])
    maskT_d = din("maskT", [128, 128])
    bands_d = din("bands", [128, 12 * 128], BF16)
    invcnt0_d = din("invcnt0", [128, 512])
    out_d = nc.dram_tensor("out", [ntok, D], F32, kind="ExternalOutput").ap()

    P = Prog()

    def A(eng, fn, r=(), w=(), **kw):
        return P.add(eng, fn, reads=r, writes=w, **kw)

    with ExitStack() as es:
        def sb(name, shape, dt):
            return es.enter_context(nc.sbuf_tensor(name, list(shape), dt))

        wsl = [sb(f"wsl{i}", [128, 8192], BF16) for i in range(NSLOT)]
        h_t = sb("h", [128, NSUB * D], F32)
        actT_t = sb("actT", [128, KC * TT], BF16)
        W_t = sb("Wst", [128, 1024], F32)
        U_t = sb("Ust", [128, 1024], BF16)
        ulast_t = sb("ulast", [128, 1024], BF16)
        ident = sb("identsb", [128, 128], BF16)
        gtab = sb("gtabsb", [128, 24], F32)
        maskT = sb("maskTsb", [128, 128], F32)
        bands = sb("bandssb", [128, 12 * 128], BF16)
        invcnt0 = sb("invcnt0sb", [128, 512], F32)
        poolw = sb("poolwsb", [128, 4 * 2 * 256], BF16)
        nwc = sb("nwcsb", [128, 48], F32)
        plsc = sb("plscsb", [128, 8], F32)
        gnwc = sb("gnwcsb", [128, 8], F32)
        cw = sb("cwsb", [128, 86 * 3], F32)
        cb = sb("cbsb", [128, 86], F32)
        carry = sb("carrysb", [128, 86 * 2], F32)
        st_ss = sb("st_ss", [128, 64], F32)
        st_rstd = sb("st_rstd", [128, 64], F32)
        g_sum = sb("g_sum", [128, 8], F32)
        g_sq = sb("g_sq", [128, 8], F32)
        g_mean = sb("g_mean", [128, 8], F32)
        g_m2 = sb("g_m2", [128, 8], F32)
        g_var = sb("g_var", [128, 8], F32)
        g_rstd = sb("g_rstd", [128, 8], F32)
        neghalf = sb("neghalf", [128, 8], F32)
        qT2_t = sb("qT2", [128, 1024], BF16)
        PT2_t = sb("PT2", [128, 1024], BF16)
        ARENA_B = 88320
        arena = sb("arena", [128, ARENA_B // 2], BF16)
        ps = [es.enter_context(nc.psum_tensor(f"ps{i}", [128, 512], F32)) for i in range(8)]

        def carve(off_b, nbytes, dt):
            assert off_b % 64 == 0 and off_b + nbytes <= ARENA_B, (off_b, nbytes)
            v = arena[:, off_b // 2:(off_b + nbytes) // 2]
            if dt == F32:
                v = v.bitcast(F32)
            return v

        zu = carve(0, 8192, BF16).rearrange("p (s f) -> p s f", s=NSUB)
        zq = carve(8192, 8192, BF16).rearrange("p (s f) -> p s f", s=NSUB)
        zk = carve(16384, 8192, BF16).rearrange("p (s f) -> p s f", s=NSUB)
        zv = carve(24576, 8192, BF16).rearrange("p (s f) -> p s f", s=NSUB)
        zsg = carve(32768, 16384, F32).rearrange("p (s f) -> p s f", s=NSUB)
        abf = [carve(49152 + 4096 * i, 4096, BF16) for i in range(2)]
        rot_flat = carve(57344, 6144, F32)
        rot = rot_flat.rearrange("p (a s f) -> p a s f", a=6, s=NSUB)
        rt1 = carve(63488, 2048, F32)
        rt2 = carve(65536, 2048, F32)
        qT = carve(67584, 2048, BF16)
        kT = carve(69632, 2048, BF16)
        PT = carve(71680, 2048, BF16)
        r32 = carve(73728, 4096, F32)
        r32x = [r32, carve(63488, 4096, F32)]
        qTb = [qT, qT2_t[:]]
        PTb = [PT, PT2_t[:]]
        tmp32 = carve(77824, 4096, F32)
        rgbf = carve(81920, 2048, BF16)
        stt = carve(83968, 4096, F32)
        hidT = carve(0, NFC * TT * 2, BF16).rearrange("p (f t) -> p f t", f=NFC)
        ybuf = [[carve(57344 + 2112 * (2 * wch + i), 2064, F32) for i in range(2)] for wch in range(2)]
        accb = [[carve(65792 + 2048 * (2 * wch + i), 2048, F32) for i in range(2)] for wch in range(2)]
        sgt = [carve(73984 + 2048 * i, 2048, F32) for i in range(5)]
        p32 = carve(44032, 4096, F32).rearrange("p (s f) -> p s f", s=NSUB)
        pbf = carve(84224, 2048, BF16).rearrange("p (s f) -> p s f", s=NSUB)
        pT = carve(86272, 2048, BF16).rearrange("p (c t) -> p c t", c=2)
        sgm = [carve(8192 + 2048 * i, 2048, F32) for i in range(2)]
        tmp2 = [carve(12288 + 2048 * i, 2048, F32) for i in range(2)]
        wf_tab = carve(16384, 8192, F32)
        xs = [carve(0, 8192, F32), carve(57344, 8192, F32)]
        abfn = [carve(65536 + 4096 * i, 4096, BF16) for i in range(4)]
        ppw = carve(24576, 8192, BF16).rearrange("p (c n) -> p c n", c=2)
        ostage = [carve(32768 + 8192 * i, 8192, F32) for i in range(2)]

        h = h_t[:].rearrange("p (s d) -> p s d", s=NSUB)
        actT = actT_t[:].rearrange("p (c t) -> p c t", c=KC)
        Wst = W_t[:]
        Ust = U_t[:]
        ulast = ulast_t[:]
        cw3 = cw[:].rearrange("p (f j) -> p f j", j=3)
        carry3 = carry[:].rearrange("p (f j) -> p f j", j=2)
        bands3 = bands[:].rearrange("p (a t) -> p a t", a=12)
        poolw4 = poolw[:].rearrange("p (g k d) -> p g k d", g=4, k=2)

        dkeys = ["c", "pw", "x0", "x1", "x2", "x3", "xs0", "xs1", "rot", "p", "wf", "o0", "o1"] + [f"w{i}" for i in range(NSLOT)]
        sems = {}
        for e in ("pe", "act", "dve", "pool"):
            sems[("e", e)] = es.enter_context(nc.semaphore("sem_" + e))
        for k in dkeys:
            sems[("d", k)] = es.enter_context(nc.semaphore("semd_" + k))
        block = es.enter_context(nc.Block())

        bank_ctr = [0]

        def nb():
            b = bank_ctr[0] % 8
            bank_ctr[0] += 1
            return b

        def psb(b):
            return ps[b][:].bitcast(BF16)

        slab_ctr = [0]

        def load_slab(src, kc, ncols):
            slot = slab_ctr[0] % NSLOT
            slab_ctr[0] += 1
            view = wsl[slot][:, 0:kc * ncols].rearrange("p (c n) -> p c n", c=kc)
            srcv = src.rearrange("(c p) n -> p c n", p=128)
            A("pool", lambda e: e.dma_start(out=view, in_=srcv), w=[view], dkey=f"w{slot}")
            return view

        def mm(out, lhsT, rhs, start, stop):
            A("pe", lambda e: e.matmul(out, lhsT=lhsT, rhs=rhs, start=start, stop=stop), r=[lhsT, rhs], w=[out])

        def tr(out, in_):
            A("pe", lambda e: e.transpose(out=out, in_=in_, identity=ident[:]), r=[in_, ident[:]], w=[out])

        def act(out, in_, func, **kw):
            rr = [in_]
            for key in ("bias", "scale"):
                v = kw.get(key)
                if v is not None and not isinstance(v, (int, float)):
                    rr.append(v)
            ww = [out]
            if "accum_out" in kw:
                ww.append(kw["accum_out"])
            A("act", lambda e: e.activation(out=out, in_=in_, func=func, **kw), r=rr, w=ww)

        def tt(out, in0, in1, op, eng="dve"):
            A(eng, lambda e: e.tensor_tensor(out=out, in0=in0, in1=in1, op=op), r=[in0, in1], w=[out])

        def ts(out, in0, s1, s2, op0, op1=None, eng="dve"):
            rr = [in0] + [v for v in (s1, s2) if v is not None and not isinstance(v, (int, float))]
            if op1 is None:
                A(eng, lambda e: e.tensor_scalar(out=out, in0=in0, scalar1=s1, scalar2=s2, op0=op0), r=rr, w=[out])
            else:
                A(eng, lambda e: e.tensor_scalar(out=out, in0=in0, scalar1=s1, scalar2=s2, op0=op0, op1=op1), r=rr, w=[out])

        def stt_(out, in0, scalar, in1, op0, op1, eng="dve"):
            rr = [in0, in1] + ([scalar] if not isinstance(scalar, (int, float)) else [])
            A(eng, lambda e: e.scalar_tensor_tensor(out=out, in0=in0, scalar=scalar, in1=in1, op0=op0, op1=op1), r=rr, w=[out])

        def cload(dst, src):
            A("sp", lambda e: e.dma_start(out=dst, in_=src), w=[dst], dkey="c", wait_all=True)

        cload(ident[:], ident_d)
        cload(gtab[:], gtab_d)
        cload(maskT[:], maskT_d)
        cload(bands[:], bands_d)
        cload(invcnt0[:], invcnt0_d)
        cload(nwc[:], nwc_d)
        cload(plsc[:], plsc_d)
        cload(gnwc[:], gnwc_d)
        cload(cw[:], cw_d)
        cload(cb[:], cb_d)
        pwv = pool_w_d.rearrange("(g k p) d -> p g k d", g=4, k=2)
        A("pool", lambda e: e.dma_start(out=poolw4, in_=pwv), w=[poolw[:]], dkey="pw")
        A("dve", lambda e: e.memset(neghalf[:], -0.5), w=[neghalf[:]])
        A("dve", lambda e: e.memset(Wst, 0.0), w=[Wst])
        A("dve", lambda e: e.memset(Ust, 0.0), w=[Ust])
        A("dve", lambda e: e.memset(carry[:], 0.0), w=[carry[:]])

        def ssc(s):
            return st_ss[:, s * 16:s * 16 + 1]

        def rsc(s):
            return st_rstd[:, s * 16:s * 16 + 1]

        def rstd_of(s, src, junk):
            act(junk, src, AF.Square, scale=float(D ** -0.5), accum_out=ssc(s))
            ts(rsc(s), ssc(s), NORM_EPS, None, ALU.add)
            act(rsc(s), rsc(s), AF.Sqrt)
            rs = rsc(s)
            A("dve", lambda e: e.reciprocal(out=rs, in_=rs), r=[rs], w=[rs])

        def rstd_sub(s):
            rstd_of(s, h[:, s, :], abf[s % 2])

        def n1_prefetch(Tn, s):
            t0 = Tn * TT
            xsb = xs[s % 2]
            xv = x_d[t0 + s * 128:t0 + (s + 1) * 128, :]
            A("sp", lambda e: e.dma_start(out=xsb, in_=xv), w=[xsb], dkey=f"xs{s % 2}")
            rstd_of(s, xsb, abfn[s])
            ts(abfn[s], xsb, rsc(s), None, ALU.mult)

        def norm_elem(s):
            rstd_sub(s)
            ts(abf[s % 2], h[:, s, :], rsc(s), None, ALU.mult)

        def norm_tr(widx, s, ab=None):
            if ab is None:
                ab = abf[s % 2]
            for g in range(4):
                b = nb()
                pb = psb(b)
                for j in range(4):
                    c = g * 4 + j
                    tr(pb[:, j * 128:(j + 1) * 128], ab[:, c * 128:(c + 1) * 128])
                src = pb[:, 0:512].rearrange("p (c t) -> p c t", c=4)
                dst = actT[:, g * 4:(g + 1) * 4, s * 128:(s + 1) * 128]
                wb = nwc[:, widx * 16 + g * 4: widx * 16 + g * 4 + 4].unsqueeze(2).broadcast_to([128, 4, 128])
                tt(dst, src, wb, ALU.mult)

        def norm_to_T(widx):
            for s in range(NSUB):
                norm_elem(s)
                norm_tr(widx, s)

        def rotary(b, dst, s, ci, gcol):
            pv3 = ps[b][:].rearrange("p (a f) -> p a f", a=8)
            pv4 = ps[b][:].rearrange("p (hh j f) -> p hh j f", hh=4, j=2)
            t1_3 = rt1.rearrange("p (a f) -> p a f", a=8)
            t2_4 = rt2.rearrange("p (hh j f) -> p hh j f", hh=4, j=2)
            cosb = rot[:, ci, s, :].unsqueeze(1).broadcast_to([128, 8, 64])
            sinb = rot[:, ci + 1, s, :].unsqueeze(1).broadcast_to([128, 4, 64])
            nsinb = rot[:, ci + 2, s, :].unsqueeze(1).broadcast_to([128, 4, 64])
            tt(t1_3, pv3, cosb, ALU.mult)
            tt(t2_4[:, :, 0, :], pv4[:, :, 1, :], nsinb, ALU.mult)
            tt(t2_4[:, :, 1, :], pv4[:, :, 0, :], sinb, ALU.mult)
            tt(rt1, rt1, rt2, ALU.add)
            gb = gtab[:, gcol:gcol + 4].unsqueeze(2).broadcast_to([128, 4, 128])
            tt(dst.rearrange("p (hh e) -> p hh e", hh=4), rt1.rearrange("p (hh e) -> p hh e", hh=4), gb, ALU.mult)

        def in_unit(n, s, slab):
            b = nb()
            for kc in range(KC):
                mm(ps[b][:], actT[:, kc, s * 128:(s + 1) * 128], slab[:, kc, :], kc == 0, kc == KC - 1)
            if n < 2:
                act(zu[:, s, n * 512:(n + 1) * 512], ps[b][:], AF.Copy)
            elif n < 4:
                rotary(b, zq[:, s, (n - 2) * 512:(n - 1) * 512], s, 0, (n - 2) * 4)
            elif n < 6:
                rotary(b, zk[:, s, (n - 4) * 512:(n - 3) * 512], s, 3, 8 + (n - 4) * 4)
            elif n < 8:
                act(zv[:, s, (n - 6) * 512:(n - 5) * 512], ps[b][:], AF.Copy)
            else:
                act(zsg[:, s, (n - 8) * 512:(n - 7) * 512], ps[b][:], AF.Silu)

        def phase_in():
            for n in (2, 3, 4, 5, 6, 7, 8, 9):
                slab = load_slab(w_in_d[:, n * 512:(n + 1) * 512], KC, 512)
                for s in range(NSUB):
                    in_unit(n, s, slab)

        def ret_A1(s):
            bq = nb()
            for hh in range(H):
                tr(psb(bq)[:, hh * 128:(hh + 1) * 128], zq[:, s, hh * 128:(hh + 1) * 128])
            bk = nb()
            for hh in range(H):
                tr(psb(bk)[:, hh * 128:(hh + 1) * 128], zk[:, s, hh * 128:(hh + 1) * 128])
            act(qTb[s % 2], psb(bq), AF.Copy)
            act(kT, psb(bk), AF.Copy)

        def ret_A2(s):
            qTs = qTb[s % 2]
            PTs = PTb[s % 2]
            bs = [nb(), nb()]
            for hh in range(H):
                o = ps[bs[hh // 4]][:, (hh % 4) * 128:(hh % 4 + 1) * 128]
                mm(o, kT[:, hh * 128:(hh + 1) * 128], qTs[:, hh * 128:(hh + 1) * 128], True, True)
            mb = maskT[:].unsqueeze(1).broadcast_to([128, 4, 128])
            for half in range(2):
                tt(PTs[:, half * 512:(half + 1) * 512].rearrange("p (hh i) -> p hh i", hh=4),
                   ps[bs[half]][:].rearrange("p (hh i) -> p hh i", hh=4), mb, ALU.mult)

        def ret_Bpe(s):
            bo = [nb(), nb()]
            for hh in range(H):
                o = ps[bo[hh // 4]][:, (hh % 4) * 128:(hh % 4 + 1) * 128]
                mm(o, PTb[s % 2][:, hh * 128:(hh + 1) * 128], zv[:, s, hh * 128:(hh + 1) * 128], True, False)
                mm(o, qTb[s % 2][:, hh * 128:(hh + 1) * 128], Ust[:, hh * 128:(hh + 1) * 128], False, True)
            bv = [nb(), nb()]
            for hh in range(H):
                o = ps[bv[hh // 4]][:, (hh % 4) * 128:(hh % 4 + 1) * 128]
                mm(o, zk[:, s, hh * 128:(hh + 1) * 128], zv[:, s, hh * 128:(hh + 1) * 128], True, True)
            for half in range(2):
                tt(stt[:, half * 512:(half + 1) * 512], ps[bv[half]][:], Wst[:, half * 512:(half + 1) * 512], ALU.add)
            gcb = gtab[:, 16:24].unsqueeze(2).broadcast_to([128, 8, 128])
            tt(Wst.rearrange("p (hh e) -> p hh e", hh=8), stt.rearrange("p (hh e) -> p hh e", hh=8), gcb, ALU.mult)
            act(Ust, Wst, AF.Copy)
            rr = r32x[s % 2]
            for half in range(2):
                act(rr[:, half * 512:(half + 1) * 512], ps[bo[half]][:], AF.Copy)

        def ret_Bgn(s):
            rr = r32x[s % 2]
            r3 = rr.rearrange("p (hh e) -> p hh e", hh=8)
            t3 = tmp32.rearrange("p (hh e) -> p hh e", hh=8)
            A("dve", lambda e: e.tensor_reduce(out=g_sum[:], in_=r3, axis=AX.X, op=ALU.add), r=[rr], w=[g_sum[:]])
            act(tmp32, rr, AF.Square)
            A("dve", lambda e: e.tensor_reduce(out=g_sq[:], in_=t3, axis=AX.X, op=ALU.add), r=[tmp32], w=[g_sq[:]])
            ts(g_mean[:], g_sum[:], 1.0 / HD, None, ALU.mult)
            tt(g_m2[:], g_mean[:], g_mean[:], ALU.mult)
            stt_(g_var[:], g_sq[:], 1.0 / HD, g_m2[:], ALU.mult, ALU.subtract)
            ts(g_var[:], g_var[:], GN_EPS, None, ALU.add)
            act(g_rstd[:], g_var[:], AF.Sqrt)
            A("dve", lambda e: e.reciprocal(out=g_rstd[:], in_=g_rstd[:]), r=[g_rstd[:]], w=[g_rstd[:]])
            tt(t3, r3, g_mean[:].unsqueeze(2).broadcast_to([128, 8, 128]), ALU.subtract)
            tt(t3, t3, g_rstd[:].unsqueeze(2).broadcast_to([128, 8, 128]), ALU.mult)
            tt(rgbf, tmp32, zsg[:, s, :], ALU.mult)

        def ret_C(s):
            br = nb()
            for hh in range(H):
                tr(psb(br)[:, hh * 128:(hh + 1) * 128], rgbf[:, hh * 128:(hh + 1) * 128])
            tt(actT[:, 8:16, s * 128:(s + 1) * 128], psb(br).rearrange("p (hh i) -> p hh i", hh=8),
               gnwc[:].unsqueeze(2).broadcast_to([128, 8, 128]), ALU.mult)

        muT = actT[:, 0:8, :]

        def band_unit(T, s):
            first = (T == 0 and s == 0)
            b = None
            for g in range(4):
                if g % 2 == 0:
                    b = nb()
                for m in range(2):
                    c = 2 * g + m
                    o = ps[b][:, ((g % 2) * 2 + m) * 128:((g % 2) * 2 + m + 1) * 128]
                    mm(o, zu[:, s, c * 128:(c + 1) * 128], bands3[:, (8 + g) if first else g, :], True, first)
                    if not first:
                        prev = zu[:, s - 1, c * 128:(c + 1) * 128] if s > 0 else ulast[:, c * 128:(c + 1) * 128]
                        mm(o, prev, bands3[:, 4 + g, :], False, True)
                src = ps[b][:, (g % 2) * 256:(g % 2 + 1) * 256].rearrange("p (m t) -> p m t", m=2)
                dst = muT[:, 2 * g:2 * g + 2, s * 128:(s + 1) * 128]
                if first:
                    tt(dst, src, invcnt0[:, g * 128:(g + 1) * 128].unsqueeze(1).broadcast_to([128, 2, 128]), ALU.mult)
                else:
                    act(dst, src, AF.Copy, scale=1.0 / POOL_WINDOWS[g])

        def phase_mix_tail(T):
            slab0 = load_slab(w_in_d[:, 0:512], KC, 512)
            slab1 = load_slab(w_in_d[:, 512:1024], KC, 512)
            fillers = []
            for s_ in range(NSUB):
                fillers.append(("u", s_, lambda s_=s_: in_unit(0, s_, slab0)))
                fillers.append(("u", s_, lambda s_=s_: in_unit(1, s_, slab1)))
            for s_ in range(NSUB):
                fillers.append(("b", s_, lambda s_=s_: band_unit(T, s_)))
            pieces = [("A1", 0), ("A2", 0), ("A1", 1), ("A2", 1), ("Bpe", 0),
                      ("A1", 2), ("A2", 2), ("Bpe", 1), ("Bgn", 0),
                      ("A1", 3), ("A2", 3), ("Bpe", 2), ("C", 0), ("Bgn", 1),
                      ("Bpe", 3), ("C", 1), ("Bgn", 2),
                      ("C", 2), ("Bgn", 3), ("C", 3)]
            fi = 0
            u_done = [0] * NSUB

            def run_filler():
                nonlocal fi
                if fi < len(fillers):
                    kind, s_, fn = fillers[fi]
                    if kind == "u":
                        u_done[s_] += 1
                    else:
                        assert all(u == 2 for u in u_done)
                    fn()
                    fi += 1

            for kind, s_ in pieces:
                if kind == "A1":
                    ret_A1(s_)
                elif kind == "A2":
                    ret_A2(s_)
                elif kind == "Bpe":
                    ret_Bpe(s_)
                elif kind == "Bgn":
                    ret_Bgn(s_)
                    continue
                else:
                    while u_done[s_] < 2:
                        run_filler()
                    ret_C(s_)
                run_filler()
            while fi < len(fillers):
                run_filler()
            act(ulast, zu[:, NSUB - 1, :], AF.Copy)
            for g in range(4):
                bb_ = [nb(), nb()]
                for m in range(2):
                    for kc in range(2):
                        mm(ps[bb_[m]][:], poolw4[:, g, kc, m * 128:(m + 1) * 128], muT[:, 2 * g + kc, :], kc == 0, kc == 1)
                for m in range(2):
                    oc = 2 * g + m
                    act(actT[:, oc, :], ps[bb_[m]][:], AF.Copy, scale=plsc[:, oc:oc + 1])

        def phase_out():
            for n in range(4):
                slab = load_slab(w_out_d[:, n * 512:(n + 1) * 512], KC, 512)
                for s in range(NSUB):
                    b = nb()
                    for kc in range(KC):
                        mm(ps[b][:], actT[:, kc, s * 128:(s + 1) * 128], slab[:, kc, :], kc == 0, kc == KC - 1)
                    hv = h[:, s, n * 512:(n + 1) * 512]
                    tt(hv, ps[b][:], hv, ALU.add)
                    if n == 3:
                        if s >= 2:
                            norm_tr(1, s - 2)
                        norm_elem(s)
            for s in range(NSUB - 2, NSUB):
                norm_tr(1, s)

        up_ctr = [0]

        def evac_conv(fi, bank, yb, acc):
            act(yb[:, 0:2], carry3[:, fi, :], AF.Copy)
            act(yb[:, 2:514], ps[bank][:], AF.Copy)
            act(carry3[:, fi, :], yb[:, 512:514], AF.Copy)
            act(acc, ps[bank][:], AF.Identity, scale=cw3[:, fi, 2:3], bias=cb[:, fi:fi + 1])
            stt_(acc, yb[:, 1:513], cw3[:, fi, 1:2], acc, ALU.mult, ALU.add)
            stt_(acc, yb[:, 0:512], cw3[:, fi, 0:1], acc, ALU.mult, ALU.add)

        def phase_up():
            for j in range(11):
                ncol = 512 if j < 10 else DFF - 10 * 512
                nch = ncol // 128
                sg_ = load_slab(w_up_d[:, j * 512:j * 512 + ncol], KC, ncol)
                for m in range(nch):
                    f = j * 4 + m
                    bg = nb()
                    for kc in range(KC):
                        mm(ps[bg][:], sg_[:, kc, m * 128:(m + 1) * 128], actT[:, kc, :], kc == 0, kc == KC - 1)
                    evac_conv(f, bg, ybuf[0][f % 2], accb[0][f % 2])
                    act(sgt[f % 5], accb[0][f % 2], AF.Silu)
                sv_ = load_slab(w_up_d[:, DFF + j * 512:DFF + j * 512 + ncol], KC, ncol)
                for m in range(nch):
                    f = j * 4 + m
                    bv = nb()
                    for kc in range(KC):
                        mm(ps[bv][:], sv_[:, kc, m * 128:(m + 1) * 128], actT[:, kc, :], kc == 0, kc == KC - 1)
                    evac_conv(f + NFC, bv, ybuf[1][f % 2], accb[1][f % 2])
                    tt(hidT[:, f, :], accb[1][f % 2], sgt[f % 5], ALU.mult)

        def phase_down(T):
            p_prep(T)
            parts = [(0, 16), (16, 16), (32, 11)]
            for n in range(4):
                banks = [nb() for _ in range(NSUB)]
                for pi, (f0, nf) in enumerate(parts):
                    slab = load_slab(w_down_d[f0 * 128:(f0 + nf) * 128, n * 512:(n + 1) * 512], nf, 512)
                    for s in range(NSUB):
                        for fl in range(nf):
                            f = f0 + fl
                            mm(ps[banks[s]][:], hidT[:, f, s * 128:(s + 1) * 128], slab[:, fl, :], f == 0, f == NFC - 1)
                        if n == 3 and pi == 2:
                            hv = h[:, s, n * 512:(n + 1) * 512]
                            tt(hv, ps[banks[s]][:], hv, ALU.add)
                            if s >= 2:
                                norm_tr(2, s - 2)
                            norm_elem(s)
                if n < 3:
                    for s in range(NSUB):
                        hv = h[:, s, n * 512:(n + 1) * 512]
                        tt(hv, ps[banks[s]][:], hv, ALU.add)
            for s in range(NSUB - 2, NSUB):
                norm_tr(2, s)

        def p_prep(T):
            t0 = T * TT
            pv = p_d[t0:t0 + TT, :].rearrange("(s p) d -> p s d", p=128)
            A("sp", lambda e: e.dma_start(out=p32, in_=pv), w=[p32], dkey="p")
            act(pbf, p32, AF.Copy)
            for s in range(NSUB):
                b = nb()
                for k2 in range(2):
                    tr(psb(b)[:, k2 * 128:(k2 + 1) * 128], pbf[:, s, k2 * 128:(k2 + 1) * 128])
                act(pT[:, :, s * 128:(s + 1) * 128], psb(b)[:, 0:256].rearrange("p (c t) -> p c t", c=2), AF.Copy)

        def phase_ple_prep(T):
            A("sp", lambda e: e.dma_start(out=wf_tab, in_=wf_d), w=[wf_tab], dkey="wf")

        ple_ctr = [0]

        def phase_ple(T):
            for n in range(4):
                sg_ = load_slab(wg_d[:, n * 512:(n + 1) * 512], KC, 512)
                if n == 0:
                    ppsrc = wp_d.rearrange("(c p) n -> p c n", p=128)
                    A("pool", lambda e: e.dma_start(out=ppw, in_=ppsrc), w=[ppw], dkey="pw")
                for s in range(NSUB):
                    k = ple_ctr[0] % 2
                    ple_ctr[0] += 1
                    ba = nb()
                    for kc in range(KC):
                        mm(ps[ba][:], actT[:, kc, s * 128:(s + 1) * 128], sg_[:, kc, :], kc == 0, kc == KC - 1)
                    bb = nb()
                    for k2 in range(2):
                        mm(ps[bb][:], pT[:, k2, s * 128:(s + 1) * 128], ppw[:, k2, n * 512:(n + 1) * 512], k2 == 0, k2 == 1)
                    act(sgm[k], ps[ba][:], AF.Sigmoid)
                    tt(tmp2[k], sgm[k], ps[bb][:], ALU.mult)
                    hv = h[:, s, n * 512:(n + 1) * 512]
                    tt(hv, hv, tmp2[k], ALU.add)
                    if n == 1 and T + 1 < nt:
                        n1_prefetch(T + 1, s)
                    if n == 3 and T + 1 == nt:
                        final_sub(T, s)
            if T + 1 < nt:
                load_rot(T + 1)
                for s in range(NSUB):
                    norm_tr(0, s, abfn[s])
                for s in range(NSUB):
                    final_sub(T, s)

        def final_sub(T, s):
            t0 = T * TT
            rstd_sub(s)
            og = ostage[s % 2]
            stt_(og, h[:, s, :], rsc(s), wf_tab, ALU.mult, ALU.mult)
            ov = out_d[t0 + s * 128:t0 + (s + 1) * 128, :]
            A("sp", lambda e, ov=ov, og=og: e.dma_start(out=ov, in_=og), r=[og], dkey=f"o{s % 2}")
            if T + 1 < nt:
                load_x(T + 1, s)

        def load_rot(T):
            A("sp", lambda e: e.dma_start(out=rot_flat, in_=rot_d[T]), w=[rot_flat], dkey="rot")

        def load_x(T, s):
            t0 = T * TT
            xv = x_d[t0 + s * 128:t0 + (s + 1) * 128, :]
            hv = h[:, s, :]
            A("sp", lambda e: e.dma_start(out=hv, in_=xv), w=[hv], dkey=f"x{s}")

        def store(T):
            pass

        phases = ["n1", "in", "mix", "out", "up", "down", "n3", "fin"]
        last = phases.index(stop_after) if stop_after else len(phases) - 1
        for T in range(nt):
            t0 = T * TT
            if T == 0:
                for s_ in range(NSUB):
                    load_x(0, s_)
            if T == 0:
                load_rot(0)
            steps = [(lambda: norm_to_T(0)) if T == 0 else (lambda: None), phase_in, lambda T=T: phase_mix_tail(T), phase_out,
                     phase_up, lambda T=T: phase_down(T),
                     lambda T=T: phase_ple_prep(T), lambda T=T: phase_ple(T)]
            for i, st in enumerate(steps):
                if i <= last:
                    st()
            store(T)
        P.analyze()

        @block.tensor
        def _(e):
            P.emit("pe", e, sems)

        @block.scalar
        def _(e):
            P.emit("act", e, sems)

        @block.vector
        def _(e):
            P.emit("dve", e, sems)

        @block.gpsimd
        def _(e):
            P.emit("pool", e, sems)

        @block.sync
        def _(e):
            P.emit("sp", e, sems)
            e.wait_ge(sems[("d", "o0")], 16 * P.dcount["o0"])
            e.wait_ge(sems[("d", "o1")], 16 * P.dcount["o1"])
    nc._prog_stats = {"n_ops": len(P.ops), "sig": P.sig_total}
    return nc


def make_in_maps(inputs, nt=4, cores=8):
    c = make_consts(nt)
    g = lambda k: np.asarray(inputs[k], np.float32)
    conv_w = g("conv_w")[0]
    shared = {
        "w_in": np.ascontiguousarray(g("w_in")[0]),
        "pool_w": np.ascontiguousarray(g("pool_w")[0].reshape(4 * 256, 256)),
        "w_out": np.ascontiguousarray(g("w_out")[0]),
        "w_up": np.ascontiguousarray(g("w_up")[0]),
        "w_down": np.ascontiguousarray(g("w_down")[0]),
        "ple_gate_w": np.ascontiguousarray(g("ple_gate_w")[0]),
        "ple_proj_w": np.ascontiguousarray(g("ple_proj_w")[0]),
        "nwc": np.ascontiguousarray(np.concatenate([cols(g("norm1_w")[0], 16), cols(g("norm2_w")[0], 16),
                                                    cols(g("norm3_w")[0], 16)], axis=1)),
        "plsc": cols(g("pool_scale")[0], 8),
        "gnwc": cols(g("ret_gn_w")[0], 8),
        "cw": np.ascontiguousarray(conv_w.reshape(3, 86, 128).transpose(2, 1, 0).reshape(128, 86 * 3)),
        "cb": cols(g("conv_b")[0], 86),
        "wf_tab": np.ascontiguousarray(np.broadcast_to(g("final_norm_w").reshape(1, D), (128, D))),
        "ident": c["ident"], "rot": c["rot"], "gtab": c["gtab"], "maskT": c["maskT"],
        "bands": c["bands"], "invcnt0": c["invcnt0"],
    }
    x = g("x")
    p = g("p")[0]
    ntok = nt * TT
    maps = []
    for b in range(cores):
        m = dict(shared)
        m["x"] = np.ascontiguousarray(x[b, :ntok])
        m["p"] = np.ascontiguousarray(p[b, :ntok])
        maps.append(m)
    return maps


_NC_CACHE = {}


def kernel(**inputs):
    if 4 not in _NC_CACHE:
        _NC_CACHE[4] = build_nc(4)
    nc = _NC_CACHE[4]
    maps = make_in_maps(inputs, 4, 8)
    res = run_bass_kernel_spmd(nc, maps, core_ids=list(range(8)))
    out = np.stack([np.asarray(r["out"], dtype=np.float32) for r in res.results], axis=0)
    return out
```

```python
import numpy as np
import ml_dtypes
from contextlib import ExitStack
import concourse.bass as bass
import concourse.mybir as mybir
from concourse.bass_utils import run_bass_kernel_spmd

F32 = mybir.dt.float32
BF16 = mybir.dt.bfloat16
AF = mybir.ActivationFunctionType
ALU = mybir.AluOpType
AX = mybir.AxisListType

D = 2048
S = 2048
TT = 512
NSUB = 4
KC = 16
DFF = 5504
NFC = 43
H = 8
HD = 128
PLE = 256
INC = 5120
NSLOT = 3
POOL_WINDOWS = (2, 4, 8, 16)
NORM_EPS = 1e-6
GN_EPS = 1e-5
GRAN = 64


def granules(ap):
    name = ap.tensor.name
    if name.startswith("ps"):
        return {(name, -1)}
    dims = ap.ap
    esz = mybir.dt.size(ap.dtype)
    rowlen = dims[0][0]
    col0 = (ap.offset % rowlen) if rowlen > 0 else ap.offset
    free = [(s, c) for (s, c) in dims[1:] if c > 1 and s != 0]
    if not free:
        runs = [(col0, 1)]
    else:
        free.sort(key=lambda sc: -abs(sc[0]))
        inner_s, inner_c = free[-1]
        if inner_s == 1:
            outer = free[:-1]
            runlen = inner_c
        else:
            outer = free
            runlen = 1
        nrun = 1
        for s, c in outer:
            nrun *= c
        if nrun > 128:
            lo = col0
            hi = col0 + sum(s * (c - 1) for s, c in free) + 1
            runs = [(lo, hi - lo)]
        else:
            starts = [col0]
            for s, c in outer:
                starts = [b + s * i for b in starts for i in range(c)]
            runs = [(b, runlen) for b in starts]
    keys = set()
    for b, n in runs:
        g0 = (b * esz) // GRAN
        g1 = ((b + n) * esz - 1) // GRAN
        for g in range(g0, g1 + 1):
            keys.add((name, g))
    return keys


class Op:
    __slots__ = ("idx", "eng", "fn", "rk", "wk", "dkey", "dtick", "deps", "sig", "ticket", "waits")

    def __init__(self):
        self.deps = {}
        self.sig = False
        self.ticket = 0
        self.waits = []
        self.dkey = None
        self.dtick = 0


class Prog:
    ENGS = ("pe", "act", "dve", "pool", "sp")

    def __init__(self):
        self.ops = []
        self.dcount = {}
        self.wait_all_keys = set()

    def add(self, eng, fn, reads=(), writes=(), dkey=None, wait_all=False):
        op = Op()
        op.idx = len(self.ops)
        op.eng = eng
        op.fn = fn
        rk = set()
        for a in reads:
            rk |= granules(a)
        wk = set()
        for a in writes:
            wk |= granules(a)
        op.rk = rk
        op.wk = wk
        if dkey is not None:
            op.dkey = dkey
            self.dcount[dkey] = self.dcount.get(dkey, 0) + 1
            op.dtick = 16 * self.dcount[dkey]
            if wait_all:
                self.wait_all_keys.add(dkey)
        self.ops.append(op)
        return op

    def analyze(self):
        ops = self.ops
        state = {}
        for op in ops:
            deps = op.deps
            for k in op.rk:
                st = state.get(k)
                if st is not None and st[0] >= 0:
                    deps[st[0]] = True
                if st is not None and k[1] == -1:
                    for rk_, r in st[1].items():
                        if rk_ != op.eng and r not in deps:
                            deps[r] = False
            for k in op.wk:
                st = state.get(k)
                if st is not None:
                    if st[0] >= 0:
                        deps[st[0]] = True
                    for r in st[1].values():
                        if r not in deps:
                            deps[r] = False
            deps.pop(op.idx, None)
            rkey = op.eng if op.dkey is None else ("d", op.idx)
            for k in op.rk:
                st = state.get(k)
                if st is None:
                    state[k] = [-1, {rkey: op.idx}]
                else:
                    st[1][rkey] = op.idx
            for k in op.wk:
                state[k] = [op.idx, {}]
        need = []
        for op in ops:
            lst = []
            for d, strong in op.deps.items():
                Dp = ops[d]
                if Dp.dkey is not None:
                    lst.append(d)
                elif Dp.eng == op.eng:
                    if op.eng == "pe" and op.dkey is None:
                        continue
                    lst.append(d)
                    Dp.sig = True
                else:
                    lst.append(d)
                    Dp.sig = True
            need.append(lst)
        cnt = {e: 0 for e in self.ENGS}
        for op in ops:
            if op.dkey is None and op.sig:
                cnt[op.eng] += 1
                op.ticket = cnt[op.eng]
        self.sig_total = cnt
        waited = {e: {} for e in self.ENGS}
        for op, lst in zip(ops, need):
            req = {}
            for d in lst:
                Dp = ops[d]
                if Dp.dkey is not None:
                    sk = ("d", Dp.dkey)
                    val = Dp.dtick
                    if Dp.dkey in self.wait_all_keys:
                        val = 16 * self.dcount[Dp.dkey]
                else:
                    sk = ("e", Dp.eng)
                    val = Dp.ticket
                if val > req.get(sk, 0):
                    req[sk] = val
            w = waited[op.eng]
            for sk, val in req.items():
                if w.get(sk, 0) >= val:
                    continue
                w[sk] = val
                op.waits.append((sk, val))

    def emit(self, engname, eng, sems):
        for op in self.ops:
            if op.eng != engname:
                continue
            for sk, val in op.waits:
                eng.wait_ge(sems[sk], val)
            ins = op.fn(eng)
            if op.dkey is not None:
                ins.then_inc(sems[("d", op.dkey)], 16)
            elif op.sig:
                ins.then_inc(sems[("e", engname)], 1)


def make_consts(nt):
    c = {}
    c["ident"] = np.eye(128, dtype=np.float32).astype(ml_dtypes.bfloat16)
    pos = np.arange(S, dtype=np.float32)
    inv_freq = (1.0 / (np.float32(10000.0) ** (np.arange(0, HD, 2, dtype=np.float32) / np.float32(HD)))).astype(np.float32)
    ang = (pos[:, None] * inv_freq[None, :]).astype(np.float32)
    cos = np.cos(ang.astype(np.float64)).astype(np.float32)
    sin = np.sin(ang.astype(np.float64)).astype(np.float32)
    ks = np.float32(HD ** -0.5)
    tabs = [cos, sin, -sin, cos * ks, sin * ks, -sin * ks]
    rot = np.zeros((nt, 128, 6, NSUB, 64), np.float32)
    for a, tb in enumerate(tabs):
        rot[:, :, a] = tb[: nt * TT].reshape(nt, NSUB, 128, 64).transpose(0, 2, 1, 3)
    c["rot"] = rot.reshape(nt, 128, 6 * NSUB * 64)
    gamma = 1.0 - np.exp2(-5.0 - np.arange(H, dtype=np.float64))
    pidx = np.arange(128, dtype=np.float64)
    gt = np.zeros((128, 24), np.float64)
    gt[:, 0:8] = gamma[None, :] ** pidx[:, None]
    gt[:, 8:16] = gamma[None, :] ** (-pidx[:, None])
    gt[:, 16:24] = (gamma ** 128.0)[None, :]
    c["gtab"] = gt.astype(np.float32)
    j = np.arange(128)[:, None]
    i = np.arange(128)[None, :]
    c["maskT"] = (i >= j).astype(np.float32)
    bands = np.zeros((128, 12, 128), np.float32)
    tp = np.arange(128)[:, None]
    t = np.arange(128)[None, :]
    for g, w in enumerate(POOL_WINDOWS):
        cur = ((t - tp >= 0) & (t - tp <= w - 1)).astype(np.float32)
        cur_g = cur.copy()
        cur_g[np.arange(128), np.arange(128)] = 1.0 - w
        bands[:, g, :] = cur_g
        prev = (((t + 128) - tp) <= w - 1).astype(np.float32)
        bands[:, 4 + g, :] = prev
        cnt = np.minimum(np.arange(128) + 1, w).astype(np.float32)
        cur0 = cur.copy()
        cur0[np.arange(128), np.arange(128)] = 1.0 - cnt
        bands[:, 8 + g, :] = cur0
    c["bands"] = bands.reshape(128, 12 * 128).astype(ml_dtypes.bfloat16)
    inv0 = np.zeros((4, 128), np.float32)
    for g, w in enumerate(POOL_WINDOWS):
        inv0[g] = 1.0 / np.minimum(np.arange(128) + 1, w).astype(np.float32)
    c["invcnt0"] = np.ascontiguousarray(np.broadcast_to(inv0.reshape(1, 512), (128, 512))).astype(np.float32)
    return c


def cols(v, n):
    return np.ascontiguousarray(np.asarray(v, np.float32).reshape(n, 128).T)


def build_nc(nt=4, stop_after=None):
    nc = bass.Bass("TRN2", target_bir_lowering=False)
    ntok = nt * TT

    def din(name, shape, dt=F32):
        return nc.dram_tensor(name, list(shape), dt, kind="ExternalInput").ap()

    x_d = din("x", [ntok, D])
    p_d = din("p", [ntok, PLE])
    w_in_d = din("w_in", [D, INC])
    pool_w_d = din("pool_w", [4 * 256, 256])
    w_out_d = din("w_out", [D, D])
    w_up_d = din("w_up", [D, 2 * DFF])
    w_down_d = din("w_down", [DFF, D])
    wg_d = din("ple_gate_w", [D, D])
    wp_d = din("ple_proj_w", [PLE, D])
    nwc_d = din("nwc", [128, 48])
    plsc_d = din("plsc", [128, 8])
    gnwc_d = din("gnwc", [128, 8])
    cw_d = din("cw", [128, 86 * 3])
    cb_d = din("cb", [128, 86])
    wf_d = din("wf_tab", [128, D])
    ident_d = din("ident", [128, 128], BF16)
    rot_d = din("rot", [nt, 128, 6 * NSUB * 64])
    gtab_d = din("gtab", [128, 24])
    maskT_d = din("maskT", [128, 128])
    bands_d = din("bands", [128, 12 * 128], BF16)
    invcnt0_d = din("invcnt0", [128, 512])
    out_d = nc.dram_tensor("out", [ntok, D], F32, kind="ExternalOutput").ap()

    P = Prog()

    def A(eng, fn, r=(), w=(), **kw):
        return P.add(eng, fn, reads=r, writes=w, **kw)

    with ExitStack() as es:
        def sb(name, shape, dt):
            return es.enter_context(nc.sbuf_tensor(name, list(shape), dt))

        wsl = [sb(f"wsl{i}", [128, 8192], BF16) for i in range(NSLOT)]
        h_t = sb("h", [128, NSUB * D], F32)
        actT_t = sb("actT", [128, KC * TT], BF16)
        W_t = sb("Wst", [128, 1024], F32)
        U_t = sb("Ust", [128, 1024], BF16)
        ulast_t = sb("ulast", [128, 1024], BF16)
        ident = sb("identsb", [128, 128], BF16)
        gtab = sb("gtabsb", [128, 24], F32)
        maskT = sb("maskTsb", [128, 128], F32)
        bands = sb("bandssb", [128, 12 * 128], BF16)
        invcnt0 = sb("invcnt0sb", [128, 512], F32)
        poolw = sb("poolwsb", [128, 4 * 2 * 256], BF16)
        nwc = sb("nwcsb", [128, 48], F32)
        plsc = sb("plscsb", [128, 8], F32)
        gnwc = sb("gnwcsb", [128, 8], F32)
        cw = sb("cwsb", [128, 86 * 3], F32)
        cb = sb("cbsb", [128, 86], F32)
        carry = sb("carrysb", [128, 86 * 2], F32)
        st_ss = sb("st_ss", [128, 64], F32)
        st_rstd = sb("st_rstd", [128, 64], F32)
        g_sum = sb("g_sum", [128, 8], F32)
        g_sq = sb("g_sq", [128, 8], F32)
        g_mean = sb("g_mean", [128, 8], F32)
        g_m2 = sb("g_m2", [128, 8], F32)
        g_var = sb("g_var", [128, 8], F32)
        g_rstd = sb("g_rstd", [128, 8], F32)
        neghalf = sb("neghalf", [128, 8], F32)
        ARENA_B = 88320
        arena = sb("arena", [128, ARENA_B // 2], BF16)
        ps = [es.enter_context(nc.psum_tensor(f"ps{i}", [128, 512], F32)) for i in range(8)]

        def carve(off_b, nbytes, dt):
            assert off_b % 64 == 0 and off_b + nbytes <= ARENA_B, (off_b, nbytes)
            v = arena[:, off_b // 2:(off_b + nbytes) // 2]
            if dt == F32:
                v = v.bitcast(F32)
            return v

        zu = carve(0, 8192, BF16).rearrange("p (s f) -> p s f", s=NSUB)
        zq = carve(8192, 8192, BF16).rearrange("p (s f) -> p s f", s=NSUB)
        zk = carve(16384, 8192, BF16).rearrange("p (s f) -> p s f", s=NSUB)
        zv = carve(24576, 8192, BF16).rearrange("p (s f) -> p s f", s=NSUB)
        zsg = carve(32768, 16384, F32).rearrange("p (s f) -> p s f", s=NSUB)
        abf = [carve(49152 + 4096 * i, 4096, BF16) for i in range(2)]
        rot_flat = carve(57344, 6144, F32)
        rot = rot_flat.rearrange("p (a s f) -> p a s f", a=6, s=NSUB)
        rt1 = carve(63488, 2048, F32)
        rt2 = carve(65536, 2048, F32)
        qT = carve(67584, 2048, BF16)
        kT = carve(69632, 2048, BF16)
        PT = carve(71680, 2048, BF16)
        r32 = carve(73728, 4096, F32)
        r32x = [r32, carve(63488, 4096, F32)]
        tmp32 = carve(77824, 4096, F32)
        rgbf = carve(81920, 2048, BF16)
        stt = carve(83968, 4096, F32)
        hidT = carve(0, NFC * TT * 2, BF16).rearrange("p (f t) -> p f t", f=NFC)
        ybuf = [[carve(57344 + 2112 * (2 * wch + i), 2064, F32) for i in range(2)] for wch in range(2)]
        accb = [[carve(65792 + 2048 * (2 * wch + i), 2048, F32) for i in range(2)] for wch in range(2)]
        sgt = [carve(73984 + 2048 * i, 2048, F32) for i in range(5)]
        p32 = carve(44032, 4096, F32).rearrange("p (s f) -> p s f", s=NSUB)
        pbf = carve(84224, 2048, BF16).rearrange("p (s f) -> p s f", s=NSUB)
        pT = carve(86272, 2048, BF16).rearrange("p (c t) -> p c t", c=2)
        sgm = [carve(8192 + 2048 * i, 2048, F32) for i in range(2)]
        tmp2 = [carve(12288 + 2048 * i, 2048, F32) for i in range(2)]
        wf_tab = carve(16384, 8192, F32)
        xs = [carve(0, 8192, F32), carve(57344, 8192, F32)]
        abfn = [carve(65536 + 4096 * i, 4096, BF16) for i in range(4)]
        ppw = carve(24576, 8192, BF16).rearrange("p (c n) -> p c n", c=2)
        ostage = [carve(32768 + 8192 * i, 8192, F32) for i in range(2)]

        h = h_t[:].rearrange("p (s d) -> p s d", s=NSUB)
        actT = actT_t[:].rearrange("p (c t) -> p c t", c=KC)
        Wst = W_t[:]
        Ust = U_t[:]
        ulast = ulast_t[:]
        cw3 = cw[:].rearrange("p (f j) -> p f j", j=3)
        carry3 = carry[:].rearrange("p (f j) -> p f j", j=2)
        bands3 = bands[:].rearrange("p (a t) -> p a t", a=12)
        poolw4 = poolw[:].rearrange("p (g k d) -> p g k d", g=4, k=2)

        dkeys = ["c", "pw", "x0", "x1", "x2", "x3", "xs0", "xs1", "rot", "p", "wf", "o0", "o1"] + [f"w{i}" for i in range(NSLOT)]
        sems = {}
        for e in ("pe", "act", "dve", "pool"):
            sems[("e", e)] = es.enter_context(nc.semaphore("sem_" + e))
        for k in dkeys:
            sems[("d", k)] = es.enter_context(nc.semaphore("semd_" + k))
        block = es.enter_context(nc.Block())

        bank_ctr = [0]

        def nb():
            b = bank_ctr[0] % 8
            bank_ctr[0] += 1
            return b

        def psb(b):
            return ps[b][:].bitcast(BF16)

        slab_ctr = [0]

        def load_slab(src, kc, ncols):
            slot = slab_ctr[0] % NSLOT
            slab_ctr[0] += 1
            view = wsl[slot][:, 0:kc * ncols].rearrange("p (c n) -> p c n", c=kc)
            srcv = src.rearrange("(c p) n -> p c n", p=128)
            A("pool", lambda e: e.dma_start(out=view, in_=srcv), w=[view], dkey=f"w{slot}")
            return view

        def mm(out, lhsT, rhs, start, stop):
            A("pe", lambda e: e.matmul(out, lhsT=lhsT, rhs=rhs, start=start, stop=stop), r=[lhsT, rhs], w=[out])

        def tr(out, in_):
            A("pe", lambda e: e.transpose(out=out, in_=in_, identity=ident[:]), r=[in_, ident[:]], w=[out])

        def act(out, in_, func, **kw):
            rr = [in_]
            for key in ("bias", "scale"):
                v = kw.get(key)
                if v is not None and not isinstance(v, (int, float)):
                    rr.append(v)
            ww = [out]
            if "accum_out" in kw:
                ww.append(kw["accum_out"])
            A("act", lambda e: e.activation(out=out, in_=in_, func=func, **kw), r=rr, w=ww)

        def tt(out, in0, in1, op, eng="dve"):
            A(eng, lambda e: e.tensor_tensor(out=out, in0=in0, in1=in1, op=op), r=[in0, in1], w=[out])

        def ts(out, in0, s1, s2, op0, op1=None, eng="dve"):
            rr = [in0] + [v for v in (s1, s2) if v is not None and not isinstance(v, (int, float))]
            if op1 is None:
                A(eng, lambda e: e.tensor_scalar(out=out, in0=in0, scalar1=s1, scalar2=s2, op0=op0), r=rr, w=[out])
            else:
                A(eng, lambda e: e.tensor_scalar(out=out, in0=in0, scalar1=s1, scalar2=s2, op0=op0, op1=op1), r=rr, w=[out])

        def stt_(out, in0, scalar, in1, op0, op1, eng="dve"):
            rr = [in0, in1] + ([scalar] if not isinstance(scalar, (int, float)) else [])
            A(eng, lambda e: e.scalar_tensor_tensor(out=out, in0=in0, scalar=scalar, in1=in1, op0=op0, op1=op1), r=rr, w=[out])

        def cload(dst, src):
            A("sp", lambda e: e.dma_start(out=dst, in_=src), w=[dst], dkey="c", wait_all=True)

        cload(ident[:], ident_d)
        cload(gtab[:], gtab_d)
        cload(maskT[:], maskT_d)
        cload(bands[:], bands_d)
        cload(invcnt0[:], invcnt0_d)
        cload(nwc[:], nwc_d)
        cload(plsc[:], plsc_d)
        cload(gnwc[:], gnwc_d)
        cload(cw[:], cw_d)
        cload(cb[:], cb_d)
        pwv = pool_w_d.rearrange("(g k p) d -> p g k d", g=4, k=2)
        A("pool", lambda e: e.dma_start(out=poolw4, in_=pwv), w=[poolw[:]], dkey="pw")
        A("dve", lambda e: e.memset(neghalf[:], -0.5), w=[neghalf[:]])
        A("dve", lambda e: e.memset(Wst, 0.0), w=[Wst])
        A("dve", lambda e: e.memset(Ust, 0.0), w=[Ust])
        A("dve", lambda e: e.memset(carry[:], 0.0), w=[carry[:]])

        def ssc(s):
            return st_ss[:, s * 16:s * 16 + 1]

        def rsc(s):
            return st_rstd[:, s * 16:s * 16 + 1]

        def rstd_of(s, src, junk):
            act(junk, src, AF.Square, scale=float(D ** -0.5), accum_out=ssc(s))
            ts(rsc(s), ssc(s), NORM_EPS, None, ALU.add)
            act(rsc(s), rsc(s), AF.Sqrt)
            rs = rsc(s)
            A("dve", lambda e: e.reciprocal(out=rs, in_=rs), r=[rs], w=[rs])

        def rstd_sub(s):
            rstd_of(s, h[:, s, :], abf[s % 2])

        def n1_prefetch(Tn, s):
            t0 = Tn * TT
            xsb = xs[s % 2]
            xv = x_d[t0 + s * 128:t0 + (s + 1) * 128, :]
            A("sp", lambda e: e.dma_start(out=xsb, in_=xv), w=[xsb], dkey=f"xs{s % 2}")
            rstd_of(s, xsb, abfn[s])
            ts(abfn[s], xsb, rsc(s), None, ALU.mult)

        def norm_elem(s):
            rstd_sub(s)
            ts(abf[s % 2], h[:, s, :], rsc(s), None, ALU.mult)

        def norm_tr(widx, s, ab=None):
            if ab is None:
                ab = abf[s % 2]
            for g in range(4):
                b = nb()
                pb = psb(b)
                for j in range(4):
                    c = g * 4 + j
                    tr(pb[:, j * 128:(j + 1) * 128], ab[:, c * 128:(c + 1) * 128])
                src = pb[:, 0:512].rearrange("p (c t) -> p c t", c=4)
                dst = actT[:, g * 4:(g + 1) * 4, s * 128:(s + 1) * 128]
                wb = nwc[:, widx * 16 + g * 4: widx * 16 + g * 4 + 4].unsqueeze(2).broadcast_to([128, 4, 128])
                tt(dst, src, wb, ALU.mult)

        def norm_to_T(widx):
            for s in range(NSUB):
                norm_elem(s)
                norm_tr(widx, s)

        def rotary(b, dst, s, ci, gcol):
            pv3 = ps[b][:].rearrange("p (a f) -> p a f", a=8)
            pv4 = ps[b][:].rearrange("p (hh j f) -> p hh j f", hh=4, j=2)
            t1_3 = rt1.rearrange("p (a f) -> p a f", a=8)
            t2_4 = rt2.rearrange("p (hh j f) -> p hh j f", hh=4, j=2)
            cosb = rot[:, ci, s, :].unsqueeze(1).broadcast_to([128, 8, 64])
            sinb = rot[:, ci + 1, s, :].unsqueeze(1).broadcast_to([128, 4, 64])
            nsinb = rot[:, ci + 2, s, :].unsqueeze(1).broadcast_to([128, 4, 64])
            tt(t1_3, pv3, cosb, ALU.mult)
            tt(t2_4[:, :, 0, :], pv4[:, :, 1, :], nsinb, ALU.mult)
            tt(t2_4[:, :, 1, :], pv4[:, :, 0, :], sinb, ALU.mult)
            tt(rt1, rt1, rt2, ALU.add)
            gb = gtab[:, gcol:gcol + 4].unsqueeze(2).broadcast_to([128, 4, 128])
            tt(dst.rearrange("p (hh e) -> p hh e", hh=4), rt1.rearrange("p (hh e) -> p hh e", hh=4), gb, ALU.mult)

        def in_unit(n, s, slab):
            b = nb()
            for kc in range(KC):
                mm(ps[b][:], actT[:, kc, s * 128:(s + 1) * 128], slab[:, kc, :], kc == 0, kc == KC - 1)
            if n < 2:
                act(zu[:, s, n * 512:(n + 1) * 512], ps[b][:], AF.Copy)
            elif n < 4:
                rotary(b, zq[:, s, (n - 2) * 512:(n - 1) * 512], s, 0, (n - 2) * 4)
            elif n < 6:
                rotary(b, zk[:, s, (n - 4) * 512:(n - 3) * 512], s, 3, 8 + (n - 4) * 4)
            elif n < 8:
                act(zv[:, s, (n - 6) * 512:(n - 5) * 512], ps[b][:], AF.Copy)
            else:
                act(zsg[:, s, (n - 8) * 512:(n - 7) * 512], ps[b][:], AF.Silu)

        def phase_in():
            for n in (2, 3, 4, 5, 6, 7, 8, 9):
                slab = load_slab(w_in_d[:, n * 512:(n + 1) * 512], KC, 512)
                for s in range(NSUB):
                    in_unit(n, s, slab)

        def ret_A1(s):
            bq = nb()
            for hh in range(H):
                tr(psb(bq)[:, hh * 128:(hh + 1) * 128], zq[:, s, hh * 128:(hh + 1) * 128])
            bk = nb()
            for hh in range(H):
                tr(psb(bk)[:, hh * 128:(hh + 1) * 128], zk[:, s, hh * 128:(hh + 1) * 128])
            act(qT, psb(bq), AF.Copy)
            act(kT, psb(bk), AF.Copy)

        def ret_A2(s):
            bs = [nb(), nb()]
            for hh in range(H):
                o = ps[bs[hh // 4]][:, (hh % 4) * 128:(hh % 4 + 1) * 128]
                mm(o, kT[:, hh * 128:(hh + 1) * 128], qT[:, hh * 128:(hh + 1) * 128], True, True)
            mb = maskT[:].unsqueeze(1).broadcast_to([128, 4, 128])
            for half in range(2):
                tt(PT[:, half * 512:(half + 1) * 512].rearrange("p (hh i) -> p hh i", hh=4),
                   ps[bs[half]][:].rearrange("p (hh i) -> p hh i", hh=4), mb, ALU.mult)

        def ret_Bpe(s):
            bo = [nb(), nb()]
            for hh in range(H):
                o = ps[bo[hh // 4]][:, (hh % 4) * 128:(hh % 4 + 1) * 128]
                mm(o, PT[:, hh * 128:(hh + 1) * 128], zv[:, s, hh * 128:(hh + 1) * 128], True, False)
                mm(o, qT[:, hh * 128:(hh + 1) * 128], Ust[:, hh * 128:(hh + 1) * 128], False, True)
            bv = [nb(), nb()]
            for hh in range(H):
                o = ps[bv[hh // 4]][:, (hh % 4) * 128:(hh % 4 + 1) * 128]
                mm(o, zk[:, s, hh * 128:(hh + 1) * 128], zv[:, s, hh * 128:(hh + 1) * 128], True, True)
            for half in range(2):
                tt(stt[:, half * 512:(half + 1) * 512], ps[bv[half]][:], Wst[:, half * 512:(half + 1) * 512], ALU.add)
            gcb = gtab[:, 16:24].unsqueeze(2).broadcast_to([128, 8, 128])
            tt(Wst.rearrange("p (hh e) -> p hh e", hh=8), stt.rearrange("p (hh e) -> p hh e", hh=8), gcb, ALU.mult)
            act(Ust, Wst, AF.Copy)
            rr = r32x[s % 2]
            for half in range(2):
                act(rr[:, half * 512:(half + 1) * 512], ps[bo[half]][:], AF.Copy)

        def ret_Bgn(s):
            rr = r32x[s % 2]
            r3 = rr.rearrange("p (hh e) -> p hh e", hh=8)
            t3 = tmp32.rearrange("p (hh e) -> p hh e", hh=8)
            A("dve", lambda e: e.tensor_reduce(out=g_sum[:], in_=r3, axis=AX.X, op=ALU.add), r=[rr], w=[g_sum[:]])
            act(tmp32, rr, AF.Square)
            A("dve", lambda e: e.tensor_reduce(out=g_sq[:], in_=t3, axis=AX.X, op=ALU.add), r=[tmp32], w=[g_sq[:]])
            ts(g_mean[:], g_sum[:], 1.0 / HD, None, ALU.mult)
            tt(g_m2[:], g_mean[:], g_mean[:], ALU.mult)
            stt_(g_var[:], g_sq[:], 1.0 / HD, g_m2[:], ALU.mult, ALU.subtract)
            ts(g_var[:], g_var[:], GN_EPS, None, ALU.add)
            act(g_rstd[:], g_var[:], AF.Sqrt)
            A("dve", lambda e: e.reciprocal(out=g_rstd[:], in_=g_rstd[:]), r=[g_rstd[:]], w=[g_rstd[:]])
            tt(t3, r3, g_mean[:].unsqueeze(2).broadcast_to([128, 8, 128]), ALU.subtract)
            tt(t3, t3, g_rstd[:].unsqueeze(2).broadcast_to([128, 8, 128]), ALU.mult)
            tt(rgbf, tmp32, zsg[:, s, :], ALU.mult)

        def ret_C(s):
            br = nb()
            for hh in range(H):
                tr(psb(br)[:, hh * 128:(hh + 1) * 128], rgbf[:, hh * 128:(hh + 1) * 128])
            tt(actT[:, 8:16, s * 128:(s + 1) * 128], psb(br).rearrange("p (hh i) -> p hh i", hh=8),
               gnwc[:].unsqueeze(2).broadcast_to([128, 8, 128]), ALU.mult)

        muT = actT[:, 0:8, :]

        def band_unit(T, s):
            first = (T == 0 and s == 0)
            b = None
            for g in range(4):
                if g % 2 == 0:
                    b = nb()
                for m in range(2):
                    c = 2 * g + m
                    o = ps[b][:, ((g % 2) * 2 + m) * 128:((g % 2) * 2 + m + 1) * 128]
                    mm(o, zu[:, s, c * 128:(c + 1) * 128], bands3[:, (8 + g) if first else g, :], True, first)
                    if not first:
                        prev = zu[:, s - 1, c * 128:(c + 1) * 128] if s > 0 else ulast[:, c * 128:(c + 1) * 128]
                        mm(o, prev, bands3[:, 4 + g, :], False, True)
                src = ps[b][:, (g % 2) * 256:(g % 2 + 1) * 256].rearrange("p (m t) -> p m t", m=2)
                dst = muT[:, 2 * g:2 * g + 2, s * 128:(s + 1) * 128]
                if first:
                    tt(dst, src, invcnt0[:, g * 128:(g + 1) * 128].unsqueeze(1).broadcast_to([128, 2, 128]), ALU.mult)
                else:
                    act(dst, src, AF.Copy, scale=1.0 / POOL_WINDOWS[g])

        def phase_mix_tail(T):
            slab0 = load_slab(w_in_d[:, 0:512], KC, 512)
            slab1 = load_slab(w_in_d[:, 512:1024], KC, 512)
            fillers = []
            for s_ in range(NSUB):
                fillers.append(("u", s_, lambda s_=s_: in_unit(0, s_, slab0)))
                fillers.append(("u", s_, lambda s_=s_: in_unit(1, s_, slab1)))
            for s_ in range(NSUB):
                fillers.append(("b", s_, lambda s_=s_: band_unit(T, s_)))
            pieces = [("A1", 0), ("A2", 0), ("Bpe", 0), ("Bgn", 0)]
            for s_ in range(1, NSUB):
                pieces += [("A1", s_), ("A2", s_), ("Bpe", s_), ("C", s_ - 1), ("Bgn", s_)]
            pieces.append(("C", NSUB - 1))
            fi = 0
            u_done = [0] * NSUB

            def run_filler():
                nonlocal fi
                if fi < len(fillers):
                    kind, s_, fn = fillers[fi]
                    if kind == "u":
                        u_done[s_] += 1
                    else:
                        assert all(u == 2 for u in u_done)
                    fn()
                    fi += 1

            for kind, s_ in pieces:
                if kind == "A1":
                    ret_A1(s_)
                elif kind == "A2":
                    ret_A2(s_)
                elif kind == "Bpe":
                    ret_Bpe(s_)
                elif kind == "Bgn":
                    ret_Bgn(s_)
                    continue
                else:
                    while u_done[s_] < 2:
                        run_filler()
                    ret_C(s_)
                run_filler()
            while fi < len(fillers):
                run_filler()
            act(ulast, zu[:, NSUB - 1, :], AF.Copy)
            for g in range(4):
                bb_ = [nb(), nb()]
                for m in range(2):
                    for kc in range(2):
                        mm(ps[bb_[m]][:], poolw4[:, g, kc, m * 128:(m + 1) * 128], muT[:, 2 * g + kc, :], kc == 0, kc == 1)
                for m in range(2):
                    oc = 2 * g + m
                    act(actT[:, oc, :], ps[bb_[m]][:], AF.Copy, scale=plsc[:, oc:oc + 1])

        def phase_out():
            for n in range(4):
                slab = load_slab(w_out_d[:, n * 512:(n + 1) * 512], KC, 512)
                for s in range(NSUB):
                    b = nb()
                    for kc in range(KC):
                        mm(ps[b][:], actT[:, kc, s * 128:(s + 1) * 128], slab[:, kc, :], kc == 0, kc == KC - 1)
                    hv = h[:, s, n * 512:(n + 1) * 512]
                    tt(hv, ps[b][:], hv, ALU.add)
                    if n == 3:
                        if s >= 2:
                            norm_tr(1, s - 2)
                        norm_elem(s)
            for s in range(NSUB - 2, NSUB):
                norm_tr(1, s)

        up_ctr = [0]

        def evac_conv(fi, bank, yb, acc):
            act(yb[:, 0:2], carry3[:, fi, :], AF.Copy)
            act(yb[:, 2:514], ps[bank][:], AF.Copy)
            act(carry3[:, fi, :], yb[:, 512:514], AF.Copy)
            act(acc, ps[bank][:], AF.Identity, scale=cw3[:, fi, 2:3], bias=cb[:, fi:fi + 1])
            stt_(acc, yb[:, 1:513], cw3[:, fi, 1:2], acc, ALU.mult, ALU.add)
            stt_(acc, yb[:, 0:512], cw3[:, fi, 0:1], acc, ALU.mult, ALU.add)

        def phase_up():
            for j in range(11):
                ncol = 512 if j < 10 else DFF - 10 * 512
                nch = ncol // 128
                sg_ = load_slab(w_up_d[:, j * 512:j * 512 + ncol], KC, ncol)
                for m in range(nch):
                    f = j * 4 + m
                    bg = nb()
                    for kc in range(KC):
                        mm(ps[bg][:], sg_[:, kc, m * 128:(m + 1) * 128], actT[:, kc, :], kc == 0, kc == KC - 1)
                    evac_conv(f, bg, ybuf[0][f % 2], accb[0][f % 2])
                    act(sgt[f % 5], accb[0][f % 2], AF.Silu)
                sv_ = load_slab(w_up_d[:, DFF + j * 512:DFF + j * 512 + ncol], KC, ncol)
                for m in range(nch):
                    f = j * 4 + m
                    bv = nb()
                    for kc in range(KC):
                        mm(ps[bv][:], sv_[:, kc, m * 128:(m + 1) * 128], actT[:, kc, :], kc == 0, kc == KC - 1)
                    evac_conv(f + NFC, bv, ybuf[1][f % 2], accb[1][f % 2])
                    tt(hidT[:, f, :], accb[1][f % 2], sgt[f % 5], ALU.mult)

        def phase_down(T):
            p_load(T)
            parts = [(0, 16), (16, 16), (32, 11)]
            for n in range(4):
                if n == 1:
                    p_tr()
                banks = [nb() for _ in range(NSUB)]
                for pi, (f0, nf) in enumerate(parts):
                    slab = load_slab(w_down_d[f0 * 128:(f0 + nf) * 128, n * 512:(n + 1) * 512], nf, 512)
                    for s in range(NSUB):
                        for fl in range(nf):
                            f = f0 + fl
                            mm(ps[banks[s]][:], hidT[:, f, s * 128:(s + 1) * 128], slab[:, fl, :], f == 0, f == NFC - 1)
                        if n == 3 and pi == 2:
                            hv = h[:, s, n * 512:(n + 1) * 512]
                            tt(hv, ps[banks[s]][:], hv, ALU.add)
                            if s >= 2:
                                norm_tr(2, s - 2)
                            norm_elem(s)
                if n < 3:
                    for s in range(NSUB):
                        hv = h[:, s, n * 512:(n + 1) * 512]
                        tt(hv, ps[banks[s]][:], hv, ALU.add)
            for s in range(NSUB - 2, NSUB):
                norm_tr(2, s)

        def p_load(T):
            t0 = T * TT
            pv = p_d[t0:t0 + TT, :].rearrange("(s p) d -> p s d", p=128)
            A("sp", lambda e: e.dma_start(out=p32, in_=pv), w=[p32], dkey="p")
            act(pbf, p32, AF.Copy)

        def p_tr():
            for s in range(NSUB):
                b = nb()
                for k2 in range(2):
                    tr(psb(b)[:, k2 * 128:(k2 + 1) * 128], pbf[:, s, k2 * 128:(k2 + 1) * 128])
                act(pT[:, :, s * 128:(s + 1) * 128], psb(b)[:, 0:256].rearrange("p (c t) -> p c t", c=2), AF.Copy)

        def phase_ple_prep(T):
            A("sp", lambda e: e.dma_start(out=wf_tab, in_=wf_d), w=[wf_tab], dkey="wf")

        ple_ctr = [0]

        def phase_ple(T):
            for n in range(4):
                sg_ = load_slab(wg_d[:, n * 512:(n + 1) * 512], KC, 512)
                if n == 0:
                    ppsrc = wp_d.rearrange("(c p) n -> p c n", p=128)
                    A("pool", lambda e: e.dma_start(out=ppw, in_=ppsrc), w=[ppw], dkey="pw")
                for s in range(NSUB):
                    k = ple_ctr[0] % 2
                    ple_ctr[0] += 1
                    ba = nb()
                    for kc in range(KC):
                        mm(ps[ba][:], actT[:, kc, s * 128:(s + 1) * 128], sg_[:, kc, :], kc == 0, kc == KC - 1)
                    bb = nb()
                    for k2 in range(2):
                        mm(ps[bb][:], pT[:, k2, s * 128:(s + 1) * 128], ppw[:, k2, n * 512:(n + 1) * 512], k2 == 0, k2 == 1)
                    act(sgm[k], ps[ba][:], AF.Sigmoid)
                    tt(tmp2[k], sgm[k], ps[bb][:], ALU.mult)
                    hv = h[:, s, n * 512:(n + 1) * 512]
                    tt(hv, hv, tmp2[k], ALU.add)
                    if n == 1 and T + 1 < nt:
                        n1_prefetch(T + 1, s)
            if T + 1 < nt:
                load_rot(T + 1)
                for s in range(NSUB):
                    norm_tr(0, s, abfn[s])
            for s in range(NSUB):
                final_sub(T, s)

        def final_sub(T, s):
            t0 = T * TT
            rstd_sub(s)
            og = ostage[s % 2]
            stt_(og, h[:, s, :], rsc(s), wf_tab, ALU.mult, ALU.mult)
            ov = out_d[t0 + s * 128:t0 + (s + 1) * 128, :]
            A("sp", lambda e, ov=ov, og=og: e.dma_start(out=ov, in_=og), r=[og], dkey=f"o{s % 2}")
            if T + 1 < nt:
                load_x(T + 1, s)

        def load_rot(T):
            A("sp", lambda e: e.dma_start(out=rot_flat, in_=rot_d[T]), w=[rot_flat], dkey="rot")

        def load_x(T, s):
            t0 = T * TT
            xv = x_d[t0 + s * 128:t0 + (s + 1) * 128, :]
            hv = h[:, s, :]
            A("sp", lambda e: e.dma_start(out=hv, in_=xv), w=[hv], dkey=f"x{s}")

        def store(T):
            pass

        phases = ["n1", "in", "mix", "out", "up", "down", "n3", "fin"]
        last = phases.index(stop_after) if stop_after else len(phases) - 1
        for T in range(nt):
            t0 = T * TT
            if T == 0:
                for s_ in range(NSUB):
                    load_x(0, s_)
            if T == 0:
                load_rot(0)
            steps = [(lambda: norm_to_T(0)) if T == 0 else (lambda: None), phase_in, lambda T=T: phase_mix_tail(T), phase_out,
                     phase_up, lambda T=T: phase_down(T),
                     lambda T=T: phase_ple_prep(T), lambda T=T: phase_ple(T)]
            for i, st in enumerate(steps):
                if i <= last:
                    st()
            store(T)
        P.analyze()

        @block.tensor
        def _(e):
            P.emit("pe", e, sems)

        @block.scalar
        def _(e):
            P.emit("act", e, sems)

        @block.vector
        def _(e):
            P.emit("dve", e, sems)

        @block.gpsimd
        def _(e):
            P.emit("pool", e, sems)

        @block.sync
        def _(e):
            P.emit("sp", e, sems)
            e.wait_ge(sems[("d", "o0")], 16 * P.dcount["o0"])
            e.wait_ge(sems[("d", "o1")], 16 * P.dcount["o1"])
    nc._prog_stats = {"n_ops": len(P.ops), "sig": P.sig_total}
    return nc


def make_in_maps(inputs, nt=4, cores=8):
    c = make_consts(nt)
    g = lambda k: np.asarray(inputs[k], np.float32)
    conv_w = g("conv_w")[0]
    shared = {
        "w_in": np.ascontiguousarray(g("w_in")[0]),
        "pool_w": np.ascontiguousarray(g("pool_w")[0].reshape(4 * 256, 256)),
        "w_out": np.ascontiguousarray(g("w_out")[0]),
        "w_up": np.ascontiguousarray(g("w_up")[0]),
        "w_down": np.ascontiguousarray(g("w_down")[0]),
        "ple_gate_w": np.ascontiguousarray(g("ple_gate_w")[0]),
        "ple_proj_w": np.ascontiguousarray(g("ple_proj_w")[0]),
        "nwc": np.ascontiguousarray(np.concatenate([cols(g("norm1_w")[0], 16), cols(g("norm2_w")[0], 16),
                                                    cols(g("norm3_w")[0], 16)], axis=1)),
        "plsc": cols(g("pool_scale")[0], 8),
        "gnwc": cols(g("ret_gn_w")[0], 8),
        "cw": np.ascontiguousarray(conv_w.reshape(3, 86, 128).transpose(2, 1, 0).reshape(128, 86 * 3)),
        "cb": cols(g("conv_b")[0], 86),
        "wf_tab": np.ascontiguousarray(np.broadcast_to(g("final_norm_w").reshape(1, D), (128, D))),
        "ident": c["ident"], "rot": c["rot"], "gtab": c["gtab"], "maskT": c["maskT"],
        "bands": c["bands"], "invcnt0": c["invcnt0"],
    }
    x = g("x")
    p = g("p")[0]
    ntok = nt * TT
    maps = []
    for b in range(cores):
        m = dict(shared)
        m["x"] = np.ascontiguousarray(x[b, :ntok])
        m["p"] = np.ascontiguousarray(p[b, :ntok])
        maps.append(m)
    return maps


_NC_CACHE = {}


def kernel(**inputs):
    if 4 not in _NC_CACHE:
        _NC_CACHE[4] = build_nc(4)
    nc = _NC_CACHE[4]
    maps = make_in_maps(inputs, 4, 8)
    res = run_bass_kernel_spmd(nc, maps, core_ids=list(range(8)))
    out = np.stack([np.asarray(r["out"], dtype=np.float32) for r in res.results], axis=0)
    return out
```

```python
import numpy as np
import ml_dtypes
from contextlib import ExitStack
import concourse.bass as bass
import concourse.mybir as mybir
from concourse.bass_utils import run_bass_kernel_spmd

F32 = mybir.dt.float32
BF16 = mybir.dt.bfloat16
AF = mybir.ActivationFunctionType
ALU = mybir.AluOpType
AX = mybir.AxisListType

D = 2048
S = 2048
TT = 512
NSUB = 4
KC = 16
DFF = 5504
NFC = 43
H = 8
HD = 128
PLE = 256
INC = 5120
NSLOT = 3
POOL_WINDOWS = (2, 4, 8, 16)
NORM_EPS = 1e-6
GN_EPS = 1e-5
GRAN = 64


def granules(ap):
    name = ap.tensor.name
    if name.startswith("ps"):
        return {(name, -1)}
    dims = ap.ap
    esz = mybir.dt.size(ap.dtype)
    rowlen = dims[0][0]
    col0 = (ap.offset % rowlen) if rowlen > 0 else ap.offset
    free = [(s, c) for (s, c) in dims[1:] if c > 1 and s != 0]
    if not free:
        runs = [(col0, 1)]
    else:
        free.sort(key=lambda sc: -abs(sc[0]))
        inner_s, inner_c = free[-1]
        if inner_s == 1:
            outer = free[:-1]
            runlen = inner_c
        else:
            outer = free
            runlen = 1
        nrun = 1
        for s, c in outer:
            nrun *= c
        if nrun > 128:
            lo = col0
            hi = col0 + sum(s * (c - 1) for s, c in free) + 1
            runs = [(lo, hi - lo)]
        else:
            starts = [col0]
            for s, c in outer:
                starts = [b + s * i for b in starts for i in range(c)]
            runs = [(b, runlen) for b in starts]
    keys = set()
    for b, n in runs:
        g0 = (b * esz) // GRAN
        g1 = ((b + n) * esz - 1) // GRAN
        for g in range(g0, g1 + 1):
            keys.add((name, g))
    return keys


class Op:
    __slots__ = ("idx", "eng", "fn", "rk", "wk", "dkey", "dtick", "deps", "sig", "ticket", "waits")

    def __init__(self):
        self.deps = {}
        self.sig = False
        self.ticket = 0
        self.waits = []
        self.dkey = None
        self.dtick = 0


class Prog:
    ENGS = ("pe", "act", "dve", "pool", "sp")

    def __init__(self):
        self.ops = []
        self.dcount = {}
        self.wait_all_keys = set()

    def add(self, eng, fn, reads=(), writes=(), dkey=None, wait_all=False):
        op = Op()
        op.idx = len(self.ops)
        op.eng = eng
        op.fn = fn
        rk = set()
        for a in reads:
            rk |= granules(a)
        wk = set()
        for a in writes:
            wk |= granules(a)
        op.rk = rk
        op.wk = wk
        if dkey is not None:
            op.dkey = dkey
            self.dcount[dkey] = self.dcount.get(dkey, 0) + 1
            op.dtick = 16 * self.dcount[dkey]
            if wait_all:
                self.wait_all_keys.add(dkey)
        self.ops.append(op)
        return op

    def analyze(self):
        ops = self.ops
        state = {}
        for op in ops:
            deps = op.deps
            for k in op.rk:
                st = state.get(k)
                if st is not None and st[0] >= 0:
                    deps[st[0]] = True
                if st is not None and k[1] == -1:
                    for rk_, r in st[1].items():
                        if rk_ != op.eng and r not in deps:
                            deps[r] = False
            for k in op.wk:
                st = state.get(k)
                if st is not None:
                    if st[0] >= 0:
                        deps[st[0]] = True
                    for r in st[1].values():
                        if r not in deps:
                            deps[r] = False
            deps.pop(op.idx, None)
            rkey = op.eng if op.dkey is None else ("d", op.idx)
            for k in op.rk:
                st = state.get(k)
                if st is None:
                    state[k] = [-1, {rkey: op.idx}]
                else:
                    st[1][rkey] = op.idx
            for k in op.wk:
                state[k] = [op.idx, {}]
        need = []
        for op in ops:
            lst = []
            for d, strong in op.deps.items():
                Dp = ops[d]
                if Dp.dkey is not None:
                    lst.append(d)
                elif Dp.eng == op.eng:
                    if op.eng == "pe" and op.dkey is None:
                        continue
                    lst.append(d)
                    Dp.sig = True
                else:
                    lst.append(d)
                    Dp.sig = True
            need.append(lst)
        cnt = {e: 0 for e in self.ENGS}
        for op in ops:
            if op.dkey is None and op.sig:
                cnt[op.eng] += 1
                op.ticket = cnt[op.eng]
        self.sig_total = cnt
        waited = {e: {} for e in self.ENGS}
        for op, lst in zip(ops, need):
            req = {}
            for d in lst:
                Dp = ops[d]
                if Dp.dkey is not None:
                    sk = ("d", Dp.dkey)
                    val = Dp.dtick
                    if Dp.dkey in self.wait_all_keys:
                        val = 16 * self.dcount[Dp.dkey]
                else:
                    sk = ("e", Dp.eng)
                    val = Dp.ticket
                if val > req.get(sk, 0):
                    req[sk] = val
            w = waited[op.eng]
            for sk, val in req.items():
                if w.get(sk, 0) >= val:
                    continue
                w[sk] = val
                op.waits.append((sk, val))

    def emit(self, engname, eng, sems):
        for op in self.ops:
            if op.eng != engname:
                continue
            for sk, val in op.waits:
                eng.wait_ge(sems[sk], val)
            ins = op.fn(eng)
            if op.dkey is not None:
                ins.then_inc(sems[("d", op.dkey)], 16)
            elif op.sig:
                ins.then_inc(sems[("e", engname)], 1)


def make_consts(nt):
    c = {}
    c["ident"] = np.eye(128, dtype=np.float32).astype(ml_dtypes.bfloat16)
    pos = np.arange(S, dtype=np.float32)
    inv_freq = (1.0 / (np.float32(10000.0) ** (np.arange(0, HD, 2, dtype=np.float32) / np.float32(HD)))).astype(np.float32)
    ang = (pos[:, None] * inv_freq[None, :]).astype(np.float32)
    cos = np.cos(ang.astype(np.float64)).astype(np.float32)
    sin = np.sin(ang.astype(np.float64)).astype(np.float32)
    ks = np.float32(HD ** -0.5)
    tabs = [cos, sin, -sin, cos * ks, sin * ks, -sin * ks]
    rot = np.zeros((nt, 128, 6, NSUB, 64), np.float32)
    for a, tb in enumerate(tabs):
        rot[:, :, a] = tb[: nt * TT].reshape(nt, NSUB, 128, 64).transpose(0, 2, 1, 3)
    c["rot"] = rot.reshape(nt, 128, 6 * NSUB * 64)
    gamma = 1.0 - np.exp2(-5.0 - np.arange(H, dtype=np.float64))
    pidx = np.arange(128, dtype=np.float64)
    gt = np.zeros((128, 24), np.float64)
    gt[:, 0:8] = gamma[None, :] ** pidx[:, None]
    gt[:, 8:16] = gamma[None, :] ** (-pidx[:, None])
    gt[:, 16:24] = (gamma ** 128.0)[None, :]
    c["gtab"] = gt.astype(np.float32)
    j = np.arange(128)[:, None]
    i = np.arange(128)[None, :]
    c["maskT"] = (i >= j).astype(np.float32)
    bands = np.zeros((128, 12, 128), np.float32)
    tp = np.arange(128)[:, None]
    t = np.arange(128)[None, :]
    for g, w in enumerate(POOL_WINDOWS):
        cur = ((t - tp >= 0) & (t - tp <= w - 1)).astype(np.float32)
        cur_g = cur.copy()
        cur_g[np.arange(128), np.arange(128)] = 1.0 - w
        bands[:, g, :] = cur_g
        prev = (((t + 128) - tp) <= w - 1).astype(np.float32)
        bands[:, 4 + g, :] = prev
        cnt = np.minimum(np.arange(128) + 1, w).astype(np.float32)
        cur0 = cur.copy()
        cur0[np.arange(128), np.arange(128)] = 1.0 - cnt
        bands[:, 8 + g, :] = cur0
    c["bands"] = bands.reshape(128, 12 * 128).astype(ml_dtypes.bfloat16)
    inv0 = np.zeros((4, 128), np.float32)
    for g, w in enumerate(POOL_WINDOWS):
        inv0[g] = 1.0 / np.minimum(np.arange(128) + 1, w).astype(np.float32)
    c["invcnt0"] = np.ascontiguousarray(np.broadcast_to(inv0.reshape(1, 512), (128, 512))).astype(np.float32)
    return c


def cols(v, n):
    return np.ascontiguousarray(np.asarray(v, np.float32).reshape(n, 128).T)


def build_nc(nt=4, stop_after=None):
    nc = bass.Bass("TRN2", target_bir_lowering=False)
    ntok = nt * TT

    def din(name, shape, dt=F32):
        return nc.dram_tensor(name, list(shape), dt, kind="ExternalInput").ap()

    x_d = din("x", [ntok, D])
    p_d = din("p", [ntok, PLE])
    w_in_d = din("w_in", [D, INC])
    pool_w_d = din("pool_w", [4 * 256, 256])
    w_out_d = din("w_out", [D, D])
    w_up_d = din("w_up", [D, 2 * DFF])
    w_down_d = din("w_down", [DFF, D])
    wg_d = din("ple_gate_w", [D, D])
    wp_d = din("ple_proj_w", [PLE, D])
    nwc_d = din("nwc", [128, 48])
    plsc_d = din("plsc", [128, 8])
    gnwc_d = din("gnwc", [128, 8])
    cw_d = din("cw", [128, 86 * 3])
    cb_d = din("cb", [128, 86])
    wf_d = din("wf_tab", [128, D])
    ident_d = din("ident", [128, 128], BF16)
    rot_d = din("rot", [nt, 128, 6 * NSUB * 64])
    gtab_d = din("gtab", [128, 24])
    maskT_d = din("maskT", [128, 128])
    bands_d = din("bands", [128, 12 * 128], BF16)
    invcnt0_d = din("invcnt0", [128, 512])
    out_d = nc.dram_tensor("out", [ntok, D], F32, kind="ExternalOutput").ap()

    P = Prog()

    def A(eng, fn, r=(), w=(), **kw):
        return P.add(eng, fn, reads=r, writes=w, **kw)

    with ExitStack() as es:
        def sb(name, shape, dt):
            return es.enter_context(nc.sbuf_tensor(name, list(shape), dt))

        wsl = [sb(f"wsl{i}", [128, 8192], BF16) for i in range(NSLOT)]
        h_t = sb("h", [128, NSUB * D], F32)
        actT_t = sb("actT", [128, KC * TT], BF16)
        W_t = sb("Wst", [128, 1024], F32)
        U_t = sb("Ust", [128, 1024], BF16)
        ulast_t = sb("ulast", [128, 1024], BF16)
        ident = sb("identsb", [128, 128], BF16)
        gtab = sb("gtabsb", [128, 24], F32)
        maskT = sb("maskTsb", [128, 128], F32)
        bands = sb("bandssb", [128, 12 * 128], BF16)
        invcnt0 = sb("invcnt0sb", [128, 512], F32)
        poolw = sb("poolwsb", [128, 4 * 2 * 256], BF16)
        nwc = sb("nwcsb", [128, 48], F32)
        plsc = sb("plscsb", [128, 8], F32)
        gnwc = sb("gnwcsb", [128, 8], F32)
        cw = sb("cwsb", [128, 86 * 3], F32)
        cb = sb("cbsb", [128, 86], F32)
        carry = sb("carrysb", [128, 86 * 2], F32)
        st_ss = sb("st_ss", [128, 64], F32)
        st_rstd = sb("st_rstd", [128, 64], F32)
        g_sum = sb("g_sum", [128, 8], F32)
        g_sq = sb("g_sq", [128, 8], F32)
        g_mean = sb("g_mean", [128, 8], F32)
        g_m2 = sb("g_m2", [128, 8], F32)
        g_var = sb("g_var", [128, 8], F32)
        g_rstd = sb("g_rstd", [128, 8], F32)
        neghalf = sb("neghalf", [128, 8], F32)
        ARENA_B = 88320
        arena = sb("arena", [128, ARENA_B // 2], BF16)
        ps = [es.enter_context(nc.psum_tensor(f"ps{i}", [128, 512], F32)) for i in range(8)]

        def carve(off_b, nbytes, dt):
            assert off_b % 64 == 0 and off_b + nbytes <= ARENA_B, (off_b, nbytes)
            v = arena[:, off_b // 2:(off_b + nbytes) // 2]
            if dt == F32:
                v = v.bitcast(F32)
            return v

        zu = carve(0, 8192, BF16).rearrange("p (s f) -> p s f", s=NSUB)
        zq = carve(8192, 8192, BF16).rearrange("p (s f) -> p s f", s=NSUB)
        zk = carve(16384, 8192, BF16).rearrange("p (s f) -> p s f", s=NSUB)
        zv = carve(24576, 8192, BF16).rearrange("p (s f) -> p s f", s=NSUB)
        zsg = carve(32768, 16384, F32).rearrange("p (s f) -> p s f", s=NSUB)
        abf = [carve(49152 + 4096 * i, 4096, BF16) for i in range(2)]
        rot_flat = carve(57344, 6144, F32)
        rot = rot_flat.rearrange("p (a s f) -> p a s f", a=6, s=NSUB)
        rt1 = carve(63488, 2048, F32)
        rt2 = carve(65536, 2048, F32)
        qT = carve(67584, 2048, BF16)
        kT = carve(69632, 2048, BF16)
        PT = carve(71680, 2048, BF16)
        r32 = carve(73728, 4096, F32)
        r32x = [r32, carve(63488, 4096, F32)]
        tmp32 = carve(77824, 4096, F32)
        rgbf = carve(81920, 2048, BF16)
        stt = carve(83968, 4096, F32)
        hidT = carve(0, NFC * TT * 2, BF16).rearrange("p (f t) -> p f t", f=NFC)
        ybuf = [[carve(57344 + 2112 * (2 * wch + i), 2064, F32) for i in range(2)] for wch in range(2)]
        accb = [[carve(65792 + 2048 * (2 * wch + i), 2048, F32) for i in range(2)] for wch in range(2)]
        sgt = [carve(73984 + 2048 * i, 2048, F32) for i in range(5)]
        p32 = carve(44032, 4096, F32).rearrange("p (s f) -> p s f", s=NSUB)
        pbf = carve(84224, 2048, BF16).rearrange("p (s f) -> p s f", s=NSUB)
        pT = carve(86272, 2048, BF16).rearrange("p (c t) -> p c t", c=2)
        sgm = [carve(8192 + 2048 * i, 2048, F32) for i in range(2)]
        tmp2 = [carve(12288 + 2048 * i, 2048, F32) for i in range(2)]
        wf_tab = carve(16384, 8192, F32)
        xs = [carve(0, 8192, F32), carve(57344, 8192, F32)]
        abfn = [carve(65536 + 4096 * i, 4096, BF16) for i in range(4)]
        ppw = carve(24576, 8192, BF16).rearrange("p (c n) -> p c n", c=2)
        ostage = [carve(32768 + 8192 * i, 8192, F32) for i in range(2)]

        h = h_t[:].rearrange("p (s d) -> p s d", s=NSUB)
        actT = actT_t[:].rearrange("p (c t) -> p c t", c=KC)
        Wst = W_t[:]
        Ust = U_t[:]
        ulast = ulast_t[:]
        cw3 = cw[:].rearrange("p (f j) -> p f j", j=3)
        carry3 = carry[:].rearrange("p (f j) -> p f j", j=2)
        bands3 = bands[:].rearrange("p (a t) -> p a t", a=12)
        poolw4 = poolw[:].rearrange("p (g k d) -> p g k d", g=4, k=2)

        dkeys = ["c", "pw", "x0", "x1", "x2", "x3", "xs0", "xs1", "rot", "p", "wf", "o0", "o1"] + [f"w{i}" for i in range(NSLOT)]
        sems = {}
        for e in ("pe", "act", "dve", "pool"):
            sems[("e", e)] = es.enter_context(nc.semaphore("sem_" + e))
        for k in dkeys:
            sems[("d", k)] = es.enter_context(nc.semaphore("semd_" + k))
        block = es.enter_context(nc.Block())

        bank_ctr = [0]

        def nb():
            b = bank_ctr[0] % 8
            bank_ctr[0] += 1
            return b

        def psb(b):
            return ps[b][:].bitcast(BF16)

        slab_ctr = [0]

        def load_slab(src, kc, ncols):
            slot = slab_ctr[0] % NSLOT
            slab_ctr[0] += 1
            view = wsl[slot][:, 0:kc * ncols].rearrange("p (c n) -> p c n", c=kc)
            srcv = src.rearrange("(c p) n -> p c n", p=128)
            A("pool", lambda e: e.dma_start(out=view, in_=srcv), w=[view], dkey=f"w{slot}")
            return view

        def mm(out, lhsT, rhs, start, stop):
            A("pe", lambda e: e.matmul(out, lhsT=lhsT, rhs=rhs, start=start, stop=stop), r=[lhsT, rhs], w=[out])

        def tr(out, in_):
            A("pe", lambda e: e.transpose(out=out, in_=in_, identity=ident[:]), r=[in_, ident[:]], w=[out])

        def act(out, in_, func, **kw):
            rr = [in_]
            for key in ("bias", "scale"):
                v = kw.get(key)
                if v is not None and not isinstance(v, (int, float)):
                    rr.append(v)
            ww = [out]
            if "accum_out" in kw:
                ww.append(kw["accum_out"])
            A("act", lambda e: e.activation(out=out, in_=in_, func=func, **kw), r=rr, w=ww)

        def tt(out, in0, in1, op, eng="dve"):
            A(eng, lambda e: e.tensor_tensor(out=out, in0=in0, in1=in1, op=op), r=[in0, in1], w=[out])

        def ts(out, in0, s1, s2, op0, op1=None, eng="dve"):
            rr = [in0] + [v for v in (s1, s2) if v is not None and not isinstance(v, (int, float))]
            if op1 is None:
                A(eng, lambda e: e.tensor_scalar(out=out, in0=in0, scalar1=s1, scalar2=s2, op0=op0), r=rr, w=[out])
            else:
                A(eng, lambda e: e.tensor_scalar(out=out, in0=in0, scalar1=s1, scalar2=s2, op0=op0, op1=op1), r=rr, w=[out])

        def stt_(out, in0, scalar, in1, op0, op1, eng="dve"):
            rr = [in0, in1] + ([scalar] if not isinstance(scalar, (int, float)) else [])
            A(eng, lambda e: e.scalar_tensor_tensor(out=out, in0=in0, scalar=scalar, in1=in1, op0=op0, op1=op1), r=rr, w=[out])

        def cload(dst, src):
            A("sp", lambda e: e.dma_start(out=dst, in_=src), w=[dst], dkey="c", wait_all=True)

        cload(ident[:], ident_d)
        cload(gtab[:], gtab_d)
        cload(maskT[:], maskT_d)
        cload(bands[:], bands_d)
        cload(invcnt0[:], invcnt0_d)
        cload(nwc[:], nwc_d)
        cload(plsc[:], plsc_d)
        cload(gnwc[:], gnwc_d)
        cload(cw[:], cw_d)
        cload(cb[:], cb_d)
        pwv = pool_w_d.rearrange("(g k p) d -> p g k d", g=4, k=2)
        A("pool", lambda e: e.dma_start(out=poolw4, in_=pwv), w=[poolw[:]], dkey="pw")
        A("dve", lambda e: e.memset(neghalf[:], -0.5), w=[neghalf[:]])
        A("dve", lambda e: e.memset(Wst, 0.0), w=[Wst])
        A("dve", lambda e: e.memset(Ust, 0.0), w=[Ust])
        A("dve", lambda e: e.memset(carry[:], 0.0), w=[carry[:]])

        def ssc(s):
            return st_ss[:, s * 16:s * 16 + 1]

        def rsc(s):
            return st_rstd[:, s * 16:s * 16 + 1]

        def rstd_of(s, src, junk):
            act(junk, src, AF.Square, scale=float(D ** -0.5), accum_out=ssc(s))
            ts(rsc(s), ssc(s), NORM_EPS, None, ALU.add)
            act(rsc(s), rsc(s), AF.Sqrt)
            rs = rsc(s)
            A("dve", lambda e: e.reciprocal(out=rs, in_=rs), r=[rs], w=[rs])

        def rstd_sub(s):
            rstd_of(s, h[:, s, :], abf[s % 2])

        def n1_prefetch(Tn, s):
            t0 = Tn * TT
            xsb = xs[s % 2]
            xv = x_d[t0 + s * 128:t0 + (s + 1) * 128, :]
            A("sp", lambda e: e.dma_start(out=xsb, in_=xv), w=[xsb], dkey=f"xs{s % 2}")
            rstd_of(s, xsb, abfn[s])
            ts(abfn[s], xsb, rsc(s), None, ALU.mult)

        def norm_elem(s):
            rstd_sub(s)
            ts(abf[s % 2], h[:, s, :], rsc(s), None, ALU.mult)

        def norm_tr(widx, s, ab=None):
            if ab is None:
                ab = abf[s % 2]
            for g in range(4):
                b = nb()
                pb = psb(b)
                for j in range(4):
                    c = g * 4 + j
                    tr(pb[:, j * 128:(j + 1) * 128], ab[:, c * 128:(c + 1) * 128])
                src = pb[:, 0:512].rearrange("p (c t) -> p c t", c=4)
                dst = actT[:, g * 4:(g + 1) * 4, s * 128:(s + 1) * 128]
                wb = nwc[:, widx * 16 + g * 4: widx * 16 + g * 4 + 4].unsqueeze(2).broadcast_to([128, 4, 128])
                tt(dst, src, wb, ALU.mult)

        def norm_to_T(widx):
            for s in range(NSUB):
                norm_elem(s)
                norm_tr(widx, s)

        def rotary(b, dst, s, ci, gcol):
            pv3 = ps[b][:].rearrange("p (a f) -> p a f", a=8)
            pv4 = ps[b][:].rearrange("p (hh j f) -> p hh j f", hh=4, j=2)
            t1_3 = rt1.rearrange("p (a f) -> p a f", a=8)
            t2_4 = rt2.rearrange("p (hh j f) -> p hh j f", hh=4, j=2)
            cosb = rot[:, ci, s, :].unsqueeze(1).broadcast_to([128, 8, 64])
            sinb = rot[:, ci + 1, s, :].unsqueeze(1).broadcast_to([128, 4, 64])
            nsinb = rot[:, ci + 2, s, :].unsqueeze(1).broadcast_to([128, 4, 64])
            tt(t1_3, pv3, cosb, ALU.mult)
            tt(t2_4[:, :, 0, :], pv4[:, :, 1, :], nsinb, ALU.mult)
            tt(t2_4[:, :, 1, :], pv4[:, :, 0, :], sinb, ALU.mult)
            tt(rt1, rt1, rt2, ALU.add)
            gb = gtab[:, gcol:gcol + 4].unsqueeze(2).broadcast_to([128, 4, 128])
            tt(dst.rearrange("p (hh e) -> p hh e", hh=4), rt1.rearrange("p (hh e) -> p hh e", hh=4), gb, ALU.mult)

        def in_unit(n, s, slab):
            b = nb()
            for kc in range(KC):
                mm(ps[b][:], actT[:, kc, s * 128:(s + 1) * 128], slab[:, kc, :], kc == 0, kc == KC - 1)
            if n < 2:
                act(zu[:, s, n * 512:(n + 1) * 512], ps[b][:], AF.Copy)
            elif n < 4:
                rotary(b, zq[:, s, (n - 2) * 512:(n - 1) * 512], s, 0, (n - 2) * 4)
            elif n < 6:
                rotary(b, zk[:, s, (n - 4) * 512:(n - 3) * 512], s, 3, 8 + (n - 4) * 4)
            elif n < 8:
                act(zv[:, s, (n - 6) * 512:(n - 5) * 512], ps[b][:], AF.Copy)
            else:
                act(zsg[:, s, (n - 8) * 512:(n - 7) * 512], ps[b][:], AF.Silu)

        def phase_in():
            for n in (2, 3, 4, 5, 6, 7, 8, 9):
                slab = load_slab(w_in_d[:, n * 512:(n + 1) * 512], KC, 512)
                for s in range(NSUB):
                    in_unit(n, s, slab)

        def ret_A1(s):
            bq = nb()
            for hh in range(H):
                tr(psb(bq)[:, hh * 128:(hh + 1) * 128], zq[:, s, hh * 128:(hh + 1) * 128])
            bk = nb()
            for hh in range(H):
                tr(psb(bk)[:, hh * 128:(hh + 1) * 128], zk[:, s, hh * 128:(hh + 1) * 128])
            act(qT, psb(bq), AF.Copy)
            act(kT, psb(bk), AF.Copy)

        def ret_A2(s):
            bs = [nb(), nb()]
            for hh in range(H):
                o = ps[bs[hh // 4]][:, (hh % 4) * 128:(hh % 4 + 1) * 128]
                mm(o, kT[:, hh * 128:(hh + 1) * 128], qT[:, hh * 128:(hh + 1) * 128], True, True)
            mb = maskT[:].unsqueeze(1).broadcast_to([128, 4, 128])
            for half in range(2):
                tt(PT[:, half * 512:(half + 1) * 512].rearrange("p (hh i) -> p hh i", hh=4),
                   ps[bs[half]][:].rearrange("p (hh i) -> p hh i", hh=4), mb, ALU.mult)

        def ret_Bpe(s):
            bo = [nb(), nb()]
            for hh in range(H):
                o = ps[bo[hh // 4]][:, (hh % 4) * 128:(hh % 4 + 1) * 128]
                mm(o, PT[:, hh * 128:(hh + 1) * 128], zv[:, s, hh * 128:(hh + 1) * 128], True, False)
                mm(o, qT[:, hh * 128:(hh + 1) * 128], Ust[:, hh * 128:(hh + 1) * 128], False, True)
            bv = [nb(), nb()]
            for hh in range(H):
                o = ps[bv[hh // 4]][:, (hh % 4) * 128:(hh % 4 + 1) * 128]
                mm(o, zk[:, s, hh * 128:(hh + 1) * 128], zv[:, s, hh * 128:(hh + 1) * 128], True, True)
            for half in range(2):
                tt(stt[:, half * 512:(half + 1) * 512], ps[bv[half]][:], Wst[:, half * 512:(half + 1) * 512], ALU.add)
            gcb = gtab[:, 16:24].unsqueeze(2).broadcast_to([128, 8, 128])
            tt(Wst.rearrange("p (hh e) -> p hh e", hh=8), stt.rearrange("p (hh e) -> p hh e", hh=8), gcb, ALU.mult)
            act(Ust, Wst, AF.Copy)
            rr = r32x[s % 2]
            for half in range(2):
                act(rr[:, half * 512:(half + 1) * 512], ps[bo[half]][:], AF.Copy)

        def ret_Bgn(s):
            rr = r32x[s % 2]
            r3 = rr.rearrange("p (hh e) -> p hh e", hh=8)
            t3 = tmp32.rearrange("p (hh e) -> p hh e", hh=8)
            A("dve", lambda e: e.tensor_reduce(out=g_sum[:], in_=r3, axis=AX.X, op=ALU.add), r=[rr], w=[g_sum[:]])
            act(tmp32, rr, AF.Square)
            A("dve", lambda e: e.tensor_reduce(out=g_sq[:], in_=t3, axis=AX.X, op=ALU.add), r=[tmp32], w=[g_sq[:]])
            ts(g_mean[:], g_sum[:], 1.0 / HD, None, ALU.mult)
            tt(g_m2[:], g_mean[:], g_mean[:], ALU.mult)
            stt_(g_var[:], g_sq[:], 1.0 / HD, g_m2[:], ALU.mult, ALU.subtract)
            ts(g_var[:], g_var[:], GN_EPS, None, ALU.add)
            act(g_rstd[:], g_var[:], AF.Sqrt)
            A("dve", lambda e: e.reciprocal(out=g_rstd[:], in_=g_rstd[:]), r=[g_rstd[:]], w=[g_rstd[:]])
            tt(t3, r3, g_mean[:].unsqueeze(2).broadcast_to([128, 8, 128]), ALU.subtract)
            tt(t3, t3, g_rstd[:].unsqueeze(2).broadcast_to([128, 8, 128]), ALU.mult)
            tt(rgbf, tmp32, zsg[:, s, :], ALU.mult)

        def ret_C(s):
            br = nb()
            for hh in range(H):
                tr(psb(br)[:, hh * 128:(hh + 1) * 128], rgbf[:, hh * 128:(hh + 1) * 128])
            tt(actT[:, 8:16, s * 128:(s + 1) * 128], psb(br).rearrange("p (hh i) -> p hh i", hh=8),
               gnwc[:].unsqueeze(2).broadcast_to([128, 8, 128]), ALU.mult)

        muT = actT[:, 0:8, :]

        def band_unit(T, s):
            first = (T == 0 and s == 0)
            b = None
            for g in range(4):
                if g % 2 == 0:
                    b = nb()
                for m in range(2):
                    c = 2 * g + m
                    o = ps[b][:, ((g % 2) * 2 + m) * 128:((g % 2) * 2 + m + 1) * 128]
                    mm(o, zu[:, s, c * 128:(c + 1) * 128], bands3[:, (8 + g) if first else g, :], True, first)
                    if not first:
                        prev = zu[:, s - 1, c * 128:(c + 1) * 128] if s > 0 else ulast[:, c * 128:(c + 1) * 128]
                        mm(o, prev, bands3[:, 4 + g, :], False, True)
                src = ps[b][:, (g % 2) * 256:(g % 2 + 1) * 256].rearrange("p (m t) -> p m t", m=2)
                dst = muT[:, 2 * g:2 * g + 2, s * 128:(s + 1) * 128]
                if first:
                    tt(dst, src, invcnt0[:, g * 128:(g + 1) * 128].unsqueeze(1).broadcast_to([128, 2, 128]), ALU.mult)
                else:
                    act(dst, src, AF.Copy, scale=1.0 / POOL_WINDOWS[g])

        def phase_mix_tail(T):
            slab0 = load_slab(w_in_d[:, 0:512], KC, 512)
            slab1 = load_slab(w_in_d[:, 512:1024], KC, 512)
            fillers = []
            for s_ in range(NSUB):
                fillers.append(("u", s_, lambda s_=s_: in_unit(0, s_, slab0)))
                fillers.append(("u", s_, lambda s_=s_: in_unit(1, s_, slab1)))
            for s_ in range(NSUB):
                fillers.append(("b", s_, lambda s_=s_: band_unit(T, s_)))
            pieces = [("A1", 0), ("A2", 0), ("Bpe", 0), ("Bgn", 0)]
            for s_ in range(1, NSUB):
                pieces += [("A1", s_), ("A2", s_), ("Bpe", s_), ("C", s_ - 1), ("Bgn", s_)]
            pieces.append(("C", NSUB - 1))
            fi = 0
            u_done = [0] * NSUB

            def run_filler():
                nonlocal fi
                if fi < len(fillers):
                    kind, s_, fn = fillers[fi]
                    if kind == "u":
                        u_done[s_] += 1
                    else:
                        assert all(u == 2 for u in u_done)
                    fn()
                    fi += 1

            for kind, s_ in pieces:
                if kind == "A1":
                    ret_A1(s_)
                elif kind == "A2":
                    ret_A2(s_)
                elif kind == "Bpe":
                    ret_Bpe(s_)
                elif kind == "Bgn":
                    ret_Bgn(s_)
                    continue
                else:
                    while u_done[s_] < 2:
                        run_filler()
                    ret_C(s_)
                run_filler()
            while fi < len(fillers):
                run_filler()
            act(ulast, zu[:, NSUB - 1, :], AF.Copy)
            for g in range(4):
                bb_ = [nb(), nb()]
                for m in range(2):
                    for kc in range(2):
                        mm(ps[bb_[m]][:], poolw4[:, g, kc, m * 128:(m + 1) * 128], muT[:, 2 * g + kc, :], kc == 0, kc == 1)
                for m in range(2):
                    oc = 2 * g + m
                    act(actT[:, oc, :], ps[bb_[m]][:], AF.Copy, scale=plsc[:, oc:oc + 1])

        def phase_out():
            for n in range(4):
                slab = load_slab(w_out_d[:, n * 512:(n + 1) * 512], KC, 512)
                for s in range(NSUB):
                    b = nb()
                    for kc in range(KC):
                        mm(ps[b][:], actT[:, kc, s * 128:(s + 1) * 128], slab[:, kc, :], kc == 0, kc == KC - 1)
                    hv = h[:, s, n * 512:(n + 1) * 512]
                    tt(hv, ps[b][:], hv, ALU.add)
                    if n == 3:
                        if s >= 2:
                            norm_tr(1, s - 2)
                        norm_elem(s)
            for s in range(NSUB - 2, NSUB):
                norm_tr(1, s)

        up_ctr = [0]

        def evac_conv(fi, bank, yb, acc):
            act(yb[:, 0:2], carry3[:, fi, :], AF.Copy)
            act(yb[:, 2:514], ps[bank][:], AF.Copy)
            act(carry3[:, fi, :], yb[:, 512:514], AF.Copy)
            act(acc, ps[bank][:], AF.Identity, scale=cw3[:, fi, 2:3], bias=cb[:, fi:fi + 1])
            stt_(acc, yb[:, 1:513], cw3[:, fi, 1:2], acc, ALU.mult, ALU.add)
            stt_(acc, yb[:, 0:512], cw3[:, fi, 0:1], acc, ALU.mult, ALU.add)

        def phase_up():
            for j in range(11):
                ncol = 512 if j < 10 else DFF - 10 * 512
                nch = ncol // 128
                sg_ = load_slab(w_up_d[:, j * 512:j * 512 + ncol], KC, ncol)
                for m in range(nch):
                    f = j * 4 + m
                    bg = nb()
                    for kc in range(KC):
                        mm(ps[bg][:], sg_[:, kc, m * 128:(m + 1) * 128], actT[:, kc, :], kc == 0, kc == KC - 1)
                    evac_conv(f, bg, ybuf[0][f % 2], accb[0][f % 2])
                    act(sgt[f % 5], accb[0][f % 2], AF.Silu)
                sv_ = load_slab(w_up_d[:, DFF + j * 512:DFF + j * 512 + ncol], KC, ncol)
                for m in range(nch):
                    f = j * 4 + m
                    bv = nb()
                    for kc in range(KC):
                        mm(ps[bv][:], sv_[:, kc, m * 128:(m + 1) * 128], actT[:, kc, :], kc == 0, kc == KC - 1)
                    evac_conv(f + NFC, bv, ybuf[1][f % 2], accb[1][f % 2])
                    tt(hidT[:, f, :], accb[1][f % 2], sgt[f % 5], ALU.mult)

        def phase_down(T):
            p_load(T)
            parts = [(0, 16), (16, 16), (32, 11)]
            for n in range(4):
                if n == 1:
                    p_tr()
                banks = [nb() for _ in range(NSUB)]
                for pi, (f0, nf) in enumerate(parts):
                    slab = load_slab(w_down_d[f0 * 128:(f0 + nf) * 128, n * 512:(n + 1) * 512], nf, 512)
                    for s in range(NSUB):
                        for fl in range(nf):
                            f = f0 + fl
                            mm(ps[banks[s]][:], hidT[:, f, s * 128:(s + 1) * 128], slab[:, fl, :], f == 0, f == NFC - 1)
                        if n == 3 and pi == 2:
                            hv = h[:, s, n * 512:(n + 1) * 512]
                            tt(hv, ps[banks[s]][:], hv, ALU.add)
                            if s >= 2:
                                norm_tr(2, s - 2)
                            norm_elem(s)
                if n < 3:
                    for s in range(NSUB):
                        hv = h[:, s, n * 512:(n + 1) * 512]
                        tt(hv, ps[banks[s]][:], hv, ALU.add)
            for s in range(NSUB - 2, NSUB):
                norm_tr(2, s)

        def p_load(T):
            t0 = T * TT
            pv = p_d[t0:t0 + TT, :].rearrange("(s p) d -> p s d", p=128)
            A("sp", lambda e: e.dma_start(out=p32, in_=pv), w=[p32], dkey="p")
            act(pbf, p32, AF.Copy)

        def p_tr():
            for s in range(NSUB):
                b = nb()
                for k2 in range(2):
                    tr(psb(b)[:, k2 * 128:(k2 + 1) * 128], pbf[:, s, k2 * 128:(k2 + 1) * 128])
                act(pT[:, :, s * 128:(s + 1) * 128], psb(b)[:, 0:256].rearrange("p (c t) -> p c t", c=2), AF.Copy)

        def phase_ple_prep(T):
            A("sp", lambda e: e.dma_start(out=wf_tab, in_=wf_d), w=[wf_tab], dkey="wf")

        ple_ctr = [0]

        def phase_ple(T):
            for n in range(4):
                sg_ = load_slab(wg_d[:, n * 512:(n + 1) * 512], KC, 512)
                if n == 0:
                    ppsrc = wp_d.rearrange("(c p) n -> p c n", p=128)
                    A("pool", lambda e: e.dma_start(out=ppw, in_=ppsrc), w=[ppw], dkey="pw")
                for s in range(NSUB):
                    k = ple_ctr[0] % 2
                    ple_ctr[0] += 1
                    ba = nb()
                    for kc in range(KC):
                        mm(ps[ba][:], actT[:, kc, s * 128:(s + 1) * 128], sg_[:, kc, :], kc == 0, kc == KC - 1)
                    bb = nb()
                    for k2 in range(2):
                        mm(ps[bb][:], pT[:, k2, s * 128:(s + 1) * 128], ppw[:, k2, n * 512:(n + 1) * 512], k2 == 0, k2 == 1)
                    act(sgm[k], ps[ba][:], AF.Sigmoid)
                    tt(tmp2[k], sgm[k], ps[bb][:], ALU.mult)
                    hv = h[:, s, n * 512:(n + 1) * 512]
                    tt(hv, hv, tmp2[k], ALU.add)
                    if n == 1 and T + 1 < nt:
                        n1_prefetch(T + 1, s)
                    if n == 3 and T + 1 == nt:
                        final_sub(T, s)
            if T + 1 < nt:
                load_rot(T + 1)
                for s in range(NSUB):
                    norm_tr(0, s, abfn[s])
                for s in range(NSUB):
                    final_sub(T, s)

        def final_sub(T, s):
            t0 = T * TT
            rstd_sub(s)
            og = ostage[s % 2]
            stt_(og, h[:, s, :], rsc(s), wf_tab, ALU.mult, ALU.mult)
            ov = out_d[t0 + s * 128:t0 + (s + 1) * 128, :]
            A("sp", lambda e, ov=ov, og=og: e.dma_start(out=ov, in_=og), r=[og], dkey=f"o{s % 2}")
            if T + 1 < nt:
                load_x(T + 1, s)

        def load_rot(T):
            A("sp", lambda e: e.dma_start(out=rot_flat, in_=rot_d[T]), w=[rot_flat], dkey="rot")

        def load_x(T, s):
            t0 = T * TT
            xv = x_d[t0 + s * 128:t0 + (s + 1) * 128, :]
            hv = h[:, s, :]
            A("sp", lambda e: e.dma_start(out=hv, in_=xv), w=[hv], dkey=f"x{s}")

        def store(T):
            pass

        phases = ["n1", "in", "mix", "out", "up", "down", "n3", "fin"]
        last = phases.index(stop_after) if stop_after else len(phases) - 1
        for T in range(nt):
            t0 = T * TT
            if T == 0:
                for s_ in range(NSUB):
                    load_x(0, s_)
            if T == 0:
                load_rot(0)
            steps = [(lambda: norm_to_T(0)) if T == 0 else (lambda: None), phase_in, lambda T=T: phase_mix_tail(T), phase_out,
                     phase_up, lambda T=T: phase_down(T),
                     lambda T=T: phase_ple_prep(T), lambda T=T: phase_ple(T)]
            for i, st in enumerate(steps):
                if i <= last:
                    st()
            store(T)
        P.analyze()

        @block.tensor
        def _(e):
            P.emit("pe", e, sems)

        @block.scalar
        def _(e):
            P.emit("act", e, sems)

        @block.vector
        def _(e):
            P.emit("dve", e, sems)

        @block.gpsimd
        def _(e):
            P.emit("pool", e, sems)

        @block.sync
        def _(e):
            P.emit("sp", e, sems)
            e.wait_ge(sems[("d", "o0")], 16 * P.dcount["o0"])
            e.wait_ge(sems[("d", "o1")], 16 * P.dcount["o1"])
    nc._prog_stats = {"n_ops": len(P.ops), "sig": P.sig_total}
    return nc


def make_in_maps(inputs, nt=4, cores=8):
    c = make_consts(nt)
    g = lambda k: np.asarray(inputs[k], np.float32)
    conv_w = g("conv_w")[0]
    shared = {
        "w_in": np.ascontiguousarray(g("w_in")[0]),
        "pool_w": np.ascontiguousarray(g("pool_w")[0].reshape(4 * 256, 256)),
        "w_out": np.ascontiguousarray(g("w_out")[0]),
        "w_up": np.ascontiguousarray(g("w_up")[0]),
        "w_down": np.ascontiguousarray(g("w_down")[0]),
        "ple_gate_w": np.ascontiguousarray(g("ple_gate_w")[0]),
        "ple_proj_w": np.ascontiguousarray(g("ple_proj_w")[0]),
        "nwc": np.ascontiguousarray(np.concatenate([cols(g("norm1_w")[0], 16), cols(g("norm2_w")[0], 16),
                                                    cols(g("norm3_w")[0], 16)], axis=1)),
        "plsc": cols(g("pool_scale")[0], 8),
        "gnwc": cols(g("ret_gn_w")[0], 8),
        "cw": np.ascontiguousarray(conv_w.reshape(3, 86, 128).transpose(2, 1, 0).reshape(128, 86 * 3)),
        "cb": cols(g("conv_b")[0], 86),
        "wf_tab": np.ascontiguousarray(np.broadcast_to(g("final_norm_w").reshape(1, D), (128, D))),
        "ident": c["ident"], "rot": c["rot"], "gtab": c["gtab"], "maskT": c["maskT"],
        "bands": c["bands"], "invcnt0": c["invcnt0"],
    }
    x = g("x")
    p = g("p")[0]
    ntok = nt * TT
    maps = []
    for b in range(cores):
        m = dict(shared)
        m["x"] = np.ascontiguousarray(x[b, :ntok])
        m["p"] = np.ascontiguousarray(p[b, :ntok])
        maps.append(m)
    return maps


_NC_CACHE = {}


def kernel(**inputs):
    if 4 not in _NC_CACHE:
        _NC_CACHE[4] = build_nc(4)
    nc = _NC_CACHE[4]
    maps = make_in_maps(inputs, 4, 8)
    res = run_bass_kernel_spmd(nc, maps, core_ids=list(range(8)))
    out = np.stack([np.asarray(r["out"], dtype=np.float32) for r in res.results], axis=0)
    return out
```

```python
import numpy as np
import ml_dtypes
from contextlib import ExitStack
import concourse.bass as bass
import concourse.mybir as mybir
from concourse.bass_utils import run_bass_kernel_spmd

F32 = mybir.dt.float32
BF16 = mybir.dt.bfloat16
AF = mybir.ActivationFunctionType
ALU = mybir.AluOpType
AX = mybir.AxisListType

D = 2048
S = 2048
TT = 512
NSUB = 4
KC = 16
DFF = 5504
NFC = 43
H = 8
HD = 128
PLE = 256
INC = 5120
NSLOT = 3
POOL_WINDOWS = (2, 4, 8, 16)
NORM_EPS = 1e-6
GN_EPS = 1e-5
GRAN = 64


def granules(ap):
    name = ap.tensor.name
    if name.startswith("ps"):
        return {(name, -1)}
    dims = ap.ap
    esz = mybir.dt.size(ap.dtype)
    rowlen = dims[0][0]
    col0 = (ap.offset % rowlen) if rowlen > 0 else ap.offset
    free = [(s, c) for (s, c) in dims[1:] if c > 1 and s != 0]
    if not free:
        runs = [(col0, 1)]
    else:
        free.sort(key=lambda sc: -abs(sc[0]))
        inner_s, inner_c = free[-1]
        if inner_s == 1:
            outer = free[:-1]
            runlen = inner_c
        else:
            outer = free
            runlen = 1
        nrun = 1
        for s, c in outer:
            nrun *= c
        if nrun > 128:
            lo = col0
            hi = col0 + sum(s * (c - 1) for s, c in free) + 1
            runs = [(lo, hi - lo)]
        else:
            starts = [col0]
            for s, c in outer:
                starts = [b + s * i for b in starts for i in range(c)]
            runs = [(b, runlen) for b in starts]
    keys = set()
    for b, n in runs:
        g0 = (b * esz) // GRAN
        g1 = ((b + n) * esz - 1) // GRAN
        for g in range(g0, g1 + 1):
            keys.add((name, g))
    return keys


class Op:
    __slots__ = ("idx", "eng", "fn", "rk", "wk", "dkey", "dtick", "deps", "sig", "ticket", "waits")

    def __init__(self):
        self.deps = {}
        self.sig = False
        self.ticket = 0
        self.waits = []
        self.dkey = None
        self.dtick = 0


class Prog:
    ENGS = ("pe", "act", "dve", "pool", "sp")

    def __init__(self):
        self.ops = []
        self.dcount = {}
        self.wait_all_keys = set()

    def add(self, eng, fn, reads=(), writes=(), dkey=None, wait_all=False):
        op = Op()
        op.idx = len(self.ops)
        op.eng = eng
        op.fn = fn
        rk = set()
        for a in reads:
            rk |= granules(a)
        wk = set()
        for a in writes:
            wk |= granules(a)
        op.rk = rk
        op.wk = wk
        if dkey is not None:
            op.dkey = dkey
            self.dcount[dkey] = self.dcount.get(dkey, 0) + 1
            op.dtick = 16 * self.dcount[dkey]
            if wait_all:
                self.wait_all_keys.add(dkey)
        self.ops.append(op)
        return op

    def analyze(self):
        ops = self.ops
        state = {}
        for op in ops:
            deps = op.deps
            for k in op.rk:
                st = state.get(k)
                if st is not None and st[0] >= 0:
                    deps[st[0]] = True
                if st is not None and k[1] == -1:
                    for rk_, r in st[1].items():
                        if rk_ != op.eng and r not in deps:
                            deps[r] = False
            for k in op.wk:
                st = state.get(k)
                if st is not None:
                    if st[0] >= 0:
                        deps[st[0]] = True
                    for r in st[1].values():
                        if r not in deps:
                            deps[r] = False
            deps.pop(op.idx, None)
            rkey = op.eng if op.dkey is None else ("d", op.idx)
            for k in op.rk:
                st = state.get(k)
                if st is None:
                    state[k] = [-1, {rkey: op.idx}]
                else:
                    st[1][rkey] = op.idx
            for k in op.wk:
                state[k] = [op.idx, {}]
        need = []
        for op in ops:
            lst = []
            for d, strong in op.deps.items():
                Dp = ops[d]
                if Dp.dkey is not None:
                    lst.append(d)
                elif Dp.eng == op.eng:
                    if op.eng == "pe" and op.dkey is None:
                        continue
                    lst.append(d)
                    Dp.sig = True
                else:
                    lst.append(d)
                    Dp.sig = True
            need.append(lst)
        cnt = {e: 0 for e in self.ENGS}
        for op in ops:
            if op.dkey is None and op.sig:
                cnt[op.eng] += 1
                op.ticket = cnt[op.eng]
        self.sig_total = cnt
        waited = {e: {} for e in self.ENGS}
        for op, lst in zip(ops, need):
            req = {}
            for d in lst:
                Dp = ops[d]
                if Dp.dkey is not None:
                    sk = ("d", Dp.dkey)
                    val = Dp.dtick
                    if Dp.dkey in self.wait_all_keys:
                        val = 16 * self.dcount[Dp.dkey]
                else:
                    sk = ("e", Dp.eng)
                    val = Dp.ticket
                if val > req.get(sk, 0):
                    req[sk] = val
            w = waited[op.eng]
            for sk, val in req.items():
                if w.get(sk, 0) >= val:
                    continue
                w[sk] = val
                op.waits.append((sk, val))

    def emit(self, engname, eng, sems):
        for op in self.ops:
            if op.eng != engname:
                continue
            for sk, val in op.waits:
                eng.wait_ge(sems[sk], val)
            ins = op.fn(eng)
            if op.dkey is not None:
                ins.then_inc(sems[("d", op.dkey)], 16)
            elif op.sig:
                ins.then_inc(sems[("e", engname)], 1)


def make_consts(nt):
    c = {}
    c["ident"] = np.eye(128, dtype=np.float32).astype(ml_dtypes.bfloat16)
    pos = np.arange(S, dtype=np.float32)
    inv_freq = (1.0 / (np.float32(10000.0) ** (np.arange(0, HD, 2, dtype=np.float32) / np.float32(HD)))).astype(np.float32)
    ang = (pos[:, None] * inv_freq[None, :]).astype(np.float32)
    cos = np.cos(ang.astype(np.float64)).astype(np.float32)
    sin = np.sin(ang.astype(np.float64)).astype(np.float32)
    ks = np.float32(HD ** -0.5)
    tabs = [cos, sin, -sin, cos * ks, sin * ks, -sin * ks]
    rot = np.zeros((nt, 128, 6, NSUB, 64), np.float32)
    for a, tb in enumerate(tabs):
        rot[:, :, a] = tb[: nt * TT].reshape(nt, NSUB, 128, 64).transpose(0, 2, 1, 3)
    c["rot"] = rot.reshape(nt, 128, 6 * NSUB * 64)
    gamma = 1.0 - np.exp2(-5.0 - np.arange(H, dtype=np.float64))
    pidx = np.arange(128, dtype=np.float64)
    gt = np.zeros((128, 24), np.float64)
    gt[:, 0:8] = gamma[None, :] ** pidx[:, None]
    gt[:, 8:16] = gamma[None, :] ** (-pidx[:, None])
    gt[:, 16:24] = (gamma ** 128.0)[None, :]
    c["gtab"] = gt.astype(np.float32)
    j = np.arange(128)[:, None]
    i = np.arange(128)[None, :]
    c["maskT"] = (i >= j).astype(np.float32)
    bands = np.zeros((128, 12, 128), np.float32)
    tp = np.arange(128)[:, None]
    t = np.arange(128)[None, :]
    for g, w in enumerate(POOL_WINDOWS):
        cur = ((t - tp >= 0) & (t - tp <= w - 1)).astype(np.float32)
        cur_g = cur.copy()
        cur_g[np.arange(128), np.arange(128)] = 1.0 - w
        bands[:, g, :] = cur_g / w
        prev = (((t + 128) - tp) <= w - 1).astype(np.float32)
        bands[:, 4 + g, :] = prev / w
        cnt = np.minimum(np.arange(128) + 1, w).astype(np.float32)
        cur0 = cur.copy()
        cur0[np.arange(128), np.arange(128)] = 1.0 - cnt
        bands[:, 8 + g, :] = cur0
    c["bands"] = bands.reshape(128, 12 * 128).astype(ml_dtypes.bfloat16)
    inv0 = np.zeros((4, 128), np.float32)
    for g, w in enumerate(POOL_WINDOWS):
        inv0[g] = 1.0 / np.minimum(np.arange(128) + 1, w).astype(np.float32)
    c["invcnt0"] = np.ascontiguousarray(np.broadcast_to(inv0.reshape(1, 512), (128, 512))).astype(np.float32)
    return c


def cols(v, n):
    return np.ascontiguousarray(np.asarray(v, np.float32).reshape(n, 128).T)


def build_nc(nt=4, stop_after=None):
    nc = bass.Bass("TRN2", target_bir_lowering=False)
    ntok = nt * TT

    def din(name, shape, dt=F32):
        return nc.dram_tensor(name, list(shape), dt, kind="ExternalInput").ap()

    x_d = din("x", [ntok, D])
    p_d = din("p", [ntok, PLE])
    w_in_d = din("w_in", [D, INC])
    pool_w_d = din("pool_w", [4 * 256, 256])
    w_out_d = din("w_out", [D, D])
    w_up_d = din("w_up", [D, 2 * DFF])
    w_down_d = din("w_down", [DFF, D])
    wg_d = din("ple_gate_w", [D, D])
    wp_d = din("ple_proj_w", [PLE, D])
    nwc_d = din("nwc", [128, 48])
    plsc_d = din("plsc", [128, 8])
    gnwc_d = din("gnwc", [128, 8])
    cw_d = din("cw", [128, 86 * 3])
    cb_d = din("cb", [128, 86])
    wf_d = din("wf_tab", [128, D])
    ident_d = din("ident", [128, 128], BF16)
    rot_d = din("rot", [nt, 128, 6 * NSUB * 64])
    gtab_d = din("gtab", [128, 24])
    maskT_d = din("maskT", [128, 128])
    bands_d = din("bands", [128, 12 * 128], BF16)
    invcnt0_d = din("invcnt0", [128, 512])
    out_d = nc.dram_tensor("out", [ntok, D], F32, kind="ExternalOutput").ap()

    P = Prog()

    def A(eng, fn, r=(), w=(), **kw):
        return P.add(eng, fn, reads=r, writes=w, **kw)

    with ExitStack() as es:
        def sb(name, shape, dt):
            return es.enter_context(nc.sbuf_tensor(name, list(shape), dt))

        wsl = [sb(f"wsl{i}", [128, 8192], BF16) for i in range(NSLOT)]
        h_t = sb("h", [128, NSUB * D], F32)
        actT_t = sb("actT", [128, KC * TT], BF16)
        W_t = sb("Wst", [128, 1024], F32)
        U_t = sb("Ust", [128, 1024], BF16)
        ulast_t = sb("ulast", [128, 1024], BF16)
        ident = sb("identsb", [128, 128], BF16)
        gtab = sb("gtabsb", [128, 24], F32)
        maskT = sb("maskTsb", [128, 128], F32)
        bands = sb("bandssb", [128, 12 * 128], BF16)
        invcnt0 = sb("invcnt0sb", [128, 512], F32)
        poolw = sb("poolwsb", [128, 4 * 2 * 256], BF16)
        nwc = sb("nwcsb", [128, 48], F32)
        plsc = sb("plscsb", [128, 8], F32)
        gnwc = sb("gnwcsb", [128, 8], F32)
        cw = sb("cwsb", [128, 86 * 3], F32)
        cb = sb("cbsb", [128, 86], F32)
        carry = sb("carrysb", [128, 86 * 2], F32)
        st_ss = sb("st_ss", [128, 64], F32)
        st_rstd = sb("st_rstd", [128, 64], F32)
        g_sum = sb("g_sum", [128, 8], F32)
        g_sq = sb("g_sq", [128, 8], F32)
        g_mean = sb("g_mean", [128, 8], F32)
        g_m2 = sb("g_m2", [128, 8], F32)
        g_var = sb("g_var", [128, 8], F32)
        g_rstd = sb("g_rstd", [128, 8], F32)
        neghalf = sb("neghalf", [128, 8], F32)
        ARENA_B = 88320
        arena = sb("arena", [128, ARENA_B // 2], BF16)
        ps = [es.enter_context(nc.psum_tensor(f"ps{i}", [128, 512], F32)) for i in range(8)]

        def carve(off_b, nbytes, dt):
            assert off_b % 64 == 0 and off_b + nbytes <= ARENA_B, (off_b, nbytes)
            v = arena[:, off_b // 2:(off_b + nbytes) // 2]
            if dt == F32:
                v = v.bitcast(F32)
            return v

        zu = carve(0, 8192, BF16).rearrange("p (s f) -> p s f", s=NSUB)
        zq = carve(8192, 8192, BF16).rearrange("p (s f) -> p s f", s=NSUB)
        zk = carve(16384, 8192, BF16).rearrange("p (s f) -> p s f", s=NSUB)
        zv = carve(24576, 8192, BF16).rearrange("p (s f) -> p s f", s=NSUB)
        zsg = carve(32768, 16384, F32).rearrange("p (s f) -> p s f", s=NSUB)
        abf = [carve(49152 + 4096 * i, 4096, BF16) for i in range(2)]
        rot_flat = carve(57344, 6144, F32)
        rot = rot_flat.rearrange("p (a s f) -> p a s f", a=6, s=NSUB)
        rt1 = carve(63488, 2048, F32)
        rt2 = carve(65536, 2048, F32)
        qT = carve(67584, 2048, BF16)
        kT = carve(69632, 2048, BF16)
        PT = carve(71680, 2048, BF16)
        r32 = carve(73728, 4096, F32)
        r32x = [r32, carve(63488, 4096, F32)]
        tmp32 = carve(77824, 4096, F32)
        rgbf = carve(81920, 2048, BF16)
        stt = carve(83968, 4096, F32)
        hidT = carve(0, NFC * TT * 2, BF16).rearrange("p (f t) -> p f t", f=NFC)
        ybuf = [[carve(57344 + 2112 * (2 * wch + i), 2064, F32) for i in range(2)] for wch in range(2)]
        accb = [[carve(65792 + 2048 * (2 * wch + i), 2048, F32) for i in range(2)] for wch in range(2)]
        sgt = [carve(73984 + 2048 * i, 2048, F32) for i in range(5)]
        p32 = carve(44032, 4096, F32).rearrange("p (s f) -> p s f", s=NSUB)
        pbf = carve(84224, 2048, BF16).rearrange("p (s f) -> p s f", s=NSUB)
        pT = carve(86272, 2048, BF16).rearrange("p (c t) -> p c t", c=2)
        sgm = [carve(8192 + 2048 * i, 2048, F32) for i in range(2)]
        tmp2 = [carve(12288 + 2048 * i, 2048, F32) for i in range(2)]
        wf_tab = carve(16384, 8192, F32)
        xs = [carve(0, 8192, F32), carve(57344, 8192, F32)]
        abfn = [carve(65536 + 4096 * i, 4096, BF16) for i in range(4)]
        ppw = carve(24576, 8192, BF16).rearrange("p (c n) -> p c n", c=2)
        ostage = [carve(32768 + 8192 * i, 8192, F32) for i in range(2)]

        h = h_t[:].rearrange("p (s d) -> p s d", s=NSUB)
        actT = actT_t[:].rearrange("p (c t) -> p c t", c=KC)
        Wst = W_t[:]
        Ust = U_t[:]
        ulast = ulast_t[:]
        cw3 = cw[:].rearrange("p (f j) -> p f j", j=3)
        carry3 = carry[:].rearrange("p (f j) -> p f j", j=2)
        bands3 = bands[:].rearrange("p (a t) -> p a t", a=12)
        poolw4 = poolw[:].rearrange("p (g k d) -> p g k d", g=4, k=2)

        dkeys = ["c", "pw", "x0", "x1", "x2", "x3", "xs0", "xs1", "rot", "p", "wf", "o0", "o1"] + [f"w{i}" for i in range(NSLOT)]
        sems = {}
        for e in ("pe", "act", "dve", "pool"):
            sems[("e", e)] = es.enter_context(nc.semaphore("sem_" + e))
        for k in dkeys:
            sems[("d", k)] = es.enter_context(nc.semaphore("semd_" + k))
        block = es.enter_context(nc.Block())

        bank_ctr = [0]

        def nb():
            b = bank_ctr[0] % 8
            bank_ctr[0] += 1
            return b

        def psb(b):
            return ps[b][:].bitcast(BF16)

        slab_ctr = [0]

        def load_slab(src, kc, ncols):
            slot = slab_ctr[0] % NSLOT
            slab_ctr[0] += 1
            view = wsl[slot][:, 0:kc * ncols].rearrange("p (c n) -> p c n", c=kc)
            srcv = src.rearrange("(c p) n -> p c n", p=128)
            A("pool", lambda e: e.dma_start(out=view, in_=srcv), w=[view], dkey=f"w{slot}")
            return view

        def mm(out, lhsT, rhs, start, stop):
            A("pe", lambda e: e.matmul(out, lhsT=lhsT, rhs=rhs, start=start, stop=stop), r=[lhsT, rhs], w=[out])

        def tr(out, in_):
            A("pe", lambda e: e.transpose(out=out, in_=in_, identity=ident[:]), r=[in_, ident[:]], w=[out])

        def act(out, in_, func, **kw):
            rr = [in_]
            for key in ("bias", "scale"):
                v = kw.get(key)
                if v is not None and not isinstance(v, (int, float)):
                    rr.append(v)
            ww = [out]
            if "accum_out" in kw:
                ww.append(kw["accum_out"])
            A("act", lambda e: e.activation(out=out, in_=in_, func=func, **kw), r=rr, w=ww)

        def tt(out, in0, in1, op, eng="dve"):
            A(eng, lambda e: e.tensor_tensor(out=out, in0=in0, in1=in1, op=op), r=[in0, in1], w=[out])

        def ts(out, in0, s1, s2, op0, op1=None, eng="dve"):
            rr = [in0] + [v for v in (s1, s2) if v is not None and not isinstance(v, (int, float))]
            if op1 is None:
                A(eng, lambda e: e.tensor_scalar(out=out, in0=in0, scalar1=s1, scalar2=s2, op0=op0), r=rr, w=[out])
            else:
                A(eng, lambda e: e.tensor_scalar(out=out, in0=in0, scalar1=s1, scalar2=s2, op0=op0, op1=op1), r=rr, w=[out])

        def stt_(out, in0, scalar, in1, op0, op1, eng="dve"):
            rr = [in0, in1] + ([scalar] if not isinstance(scalar, (int, float)) else [])
            A(eng, lambda e: e.scalar_tensor_tensor(out=out, in0=in0, scalar=scalar, in1=in1, op0=op0, op1=op1), r=rr, w=[out])

        def cload(dst, src):
            A("sp", lambda e: e.dma_start(out=dst, in_=src), w=[dst], dkey="c", wait_all=True)

        cload(ident[:], ident_d)
        cload(gtab[:], gtab_d)
        cload(maskT[:], maskT_d)
        cload(bands[:], bands_d)
        cload(invcnt0[:], invcnt0_d)
        cload(nwc[:], nwc_d)
        cload(plsc[:], plsc_d)
        cload(gnwc[:], gnwc_d)
        cload(cw[:], cw_d)
        cload(cb[:], cb_d)
        pwv = pool_w_d.rearrange("(g k p) d -> p g k d", g=4, k=2)
        A("pool", lambda e: e.dma_start(out=poolw4, in_=pwv), w=[poolw[:]], dkey="pw")
        A("dve", lambda e: e.memset(neghalf[:], -0.5), w=[neghalf[:]])
        A("dve", lambda e: e.memset(Wst, 0.0), w=[Wst])
        A("dve", lambda e: e.memset(Ust, 0.0), w=[Ust])
        A("dve", lambda e: e.memset(carry[:], 0.0), w=[carry[:]])

        def ssc(s):
            return st_ss[:, s * 16:s * 16 + 1]

        def rsc(s):
            return st_rstd[:, s * 16:s * 16 + 1]

        def rstd_of(s, src, junk):
            act(junk, src, AF.Square, scale=float(D ** -0.5), accum_out=ssc(s))
            ts(rsc(s), ssc(s), NORM_EPS, None, ALU.add)
            act(rsc(s), rsc(s), AF.Sqrt)
            rs = rsc(s)
            A("dve", lambda e: e.reciprocal(out=rs, in_=rs), r=[rs], w=[rs])

        def rstd_sub(s):
            rstd_of(s, h[:, s, :], abf[s % 2])

        def n1_prefetch(Tn, s):
            t0 = Tn * TT
            xsb = xs[s % 2]
            xv = x_d[t0 + s * 128:t0 + (s + 1) * 128, :]
            A("sp", lambda e: e.dma_start(out=xsb, in_=xv), w=[xsb], dkey=f"xs{s % 2}")
            rstd_of(s, xsb, abfn[s])
            ts(abfn[s], xsb, rsc(s), None, ALU.mult)

        def norm_elem(s):
            rstd_sub(s)
            ts(abf[s % 2], h[:, s, :], rsc(s), None, ALU.mult)

        def norm_tr(widx, s, ab=None):
            if ab is None:
                ab = abf[s % 2]
            for g in range(4):
                b = nb()
                pb = psb(b)
                for j in range(4):
                    c = g * 4 + j
                    tr(pb[:, j * 128:(j + 1) * 128], ab[:, c * 128:(c + 1) * 128])
                src = pb[:, 0:512].rearrange("p (c t) -> p c t", c=4)
                dst = actT[:, g * 4:(g + 1) * 4, s * 128:(s + 1) * 128]
                wb = nwc[:, widx * 16 + g * 4: widx * 16 + g * 4 + 4].unsqueeze(2).broadcast_to([128, 4, 128])
                tt(dst, src, wb, ALU.mult)

        def norm_to_T(widx):
            for s in range(NSUB):
                norm_elem(s)
                norm_tr(widx, s)

        def rotary(b, dst, s, ci, gcol):
            pv3 = ps[b][:].rearrange("p (a f) -> p a f", a=8)
            pv4 = ps[b][:].rearrange("p (hh j f) -> p hh j f", hh=4, j=2)
            t1_3 = rt1.rearrange("p (a f) -> p a f", a=8)
            t2_4 = rt2.rearrange("p (hh j f) -> p hh j f", hh=4, j=2)
            cosb = rot[:, ci, s, :].unsqueeze(1).broadcast_to([128, 8, 64])
            sinb = rot[:, ci + 1, s, :].unsqueeze(1).broadcast_to([128, 4, 64])
            nsinb = rot[:, ci + 2, s, :].unsqueeze(1).broadcast_to([128, 4, 64])
            tt(t1_3, pv3, cosb, ALU.mult)
            tt(t2_4[:, :, 0, :], pv4[:, :, 1, :], nsinb, ALU.mult)
            tt(t2_4[:, :, 1, :], pv4[:, :, 0, :], sinb, ALU.mult)
            tt(rt1, rt1, rt2, ALU.add)
            gb = gtab[:, gcol:gcol + 4].unsqueeze(2).broadcast_to([128, 4, 128])
            tt(dst.rearrange("p (hh e) -> p hh e", hh=4), rt1.rearrange("p (hh e) -> p hh e", hh=4), gb, ALU.mult)

        def in_unit(n, s, slab):
            b = nb()
            for kc in range(KC):
                mm(ps[b][:], actT[:, kc, s * 128:(s + 1) * 128], slab[:, kc, :], kc == 0, kc == KC - 1)
            if n < 2:
                act(zu[:, s, n * 512:(n + 1) * 512], ps[b][:], AF.Copy)
            elif n < 4:
                rotary(b, zq[:, s, (n - 2) * 512:(n - 1) * 512], s, 0, (n - 2) * 4)
            elif n < 6:
                rotary(b, zk[:, s, (n - 4) * 512:(n - 3) * 512], s, 3, 8 + (n - 4) * 4)
            elif n < 8:
                act(zv[:, s, (n - 6) * 512:(n - 5) * 512], ps[b][:], AF.Copy)
            else:
                act(zsg[:, s, (n - 8) * 512:(n - 7) * 512], ps[b][:], AF.Silu)

        def phase_in():
            for n in (2, 3, 4, 5, 6, 7, 8, 9):
                slab = load_slab(w_in_d[:, n * 512:(n + 1) * 512], KC, 512)
                for s in range(NSUB):
                    in_unit(n, s, slab)

        def ret_A1(s):
            bq = nb()
            for hh in range(H):
                tr(psb(bq)[:, hh * 128:(hh + 1) * 128], zq[:, s, hh * 128:(hh + 1) * 128])
            bk = nb()
            for hh in range(H):
                tr(psb(bk)[:, hh * 128:(hh + 1) * 128], zk[:, s, hh * 128:(hh + 1) * 128])
            act(qT, psb(bq), AF.Copy)
            act(kT, psb(bk), AF.Copy)

        def ret_A2(s):
            bs = [nb(), nb()]
            for hh in range(H):
                o = ps[bs[hh // 4]][:, (hh % 4) * 128:(hh % 4 + 1) * 128]
                mm(o, kT[:, hh * 128:(hh + 1) * 128], qT[:, hh * 128:(hh + 1) * 128], True, True)
            mb = maskT[:].unsqueeze(1).broadcast_to([128, 4, 128])
            for half in range(2):
                tt(PT[:, half * 512:(half + 1) * 512].rearrange("p (hh i) -> p hh i", hh=4),
                   ps[bs[half]][:].rearrange("p (hh i) -> p hh i", hh=4), mb, ALU.mult)

        def ret_Bpe(s):
            bo = [nb(), nb()]
            for hh in range(H):
                o = ps[bo[hh // 4]][:, (hh % 4) * 128:(hh % 4 + 1) * 128]
                mm(o, PT[:, hh * 128:(hh + 1) * 128], zv[:, s, hh * 128:(hh + 1) * 128], True, False)
                mm(o, qT[:, hh * 128:(hh + 1) * 128], Ust[:, hh * 128:(hh + 1) * 128], False, True)
            bv = [nb(), nb()]
            for hh in range(H):
                o = ps[bv[hh // 4]][:, (hh % 4) * 128:(hh % 4 + 1) * 128]
                mm(o, zk[:, s, hh * 128:(hh + 1) * 128], zv[:, s, hh * 128:(hh + 1) * 128], True, True)
            for half in range(2):
                tt(stt[:, half * 512:(half + 1) * 512], ps[bv[half]][:], Wst[:, half * 512:(half + 1) * 512], ALU.add)
            gcb = gtab[:, 16:24].unsqueeze(2).broadcast_to([128, 8, 128])
            tt(Wst.rearrange("p (hh e) -> p hh e", hh=8), stt.rearrange("p (hh e) -> p hh e", hh=8), gcb, ALU.mult)
            act(Ust, Wst, AF.Copy)
            rr = r32x[s % 2]
            for half in range(2):
                act(rr[:, half * 512:(half + 1) * 512], ps[bo[half]][:], AF.Copy)

        def ret_Bgn(s):
            rr = r32x[s % 2]
            r3 = rr.rearrange("p (hh e) -> p hh e", hh=8)
            t3 = tmp32.rearrange("p (hh e) -> p hh e", hh=8)
            A("dve", lambda e: e.tensor_reduce(out=g_sum[:], in_=r3, axis=AX.X, op=ALU.add), r=[rr], w=[g_sum[:]])
            act(tmp32, rr, AF.Square)
            A("dve", lambda e: e.tensor_reduce(out=g_sq[:], in_=t3, axis=AX.X, op=ALU.add), r=[tmp32], w=[g_sq[:]])
            ts(g_mean[:], g_sum[:], 1.0 / HD, None, ALU.mult)
            tt(g_m2[:], g_mean[:], g_mean[:], ALU.mult)
            stt_(g_var[:], g_sq[:], 1.0 / HD, g_m2[:], ALU.mult, ALU.subtract)
            ts(g_var[:], g_var[:], GN_EPS, None, ALU.add)
            act(g_rstd[:], g_var[:], AF.Sqrt)
            A("dve", lambda e: e.reciprocal(out=g_rstd[:], in_=g_rstd[:]), r=[g_rstd[:]], w=[g_rstd[:]])
            tt(t3, r3, g_mean[:].unsqueeze(2).broadcast_to([128, 8, 128]), ALU.subtract)
            tt(t3, t3, g_rstd[:].unsqueeze(2).broadcast_to([128, 8, 128]), ALU.mult)
            tt(rgbf, tmp32, zsg[:, s, :], ALU.mult)

        def ret_C(s):
            br = nb()
            for hh in range(H):
                tr(psb(br)[:, hh * 128:(hh + 1) * 128], rgbf[:, hh * 128:(hh + 1) * 128])
            tt(actT[:, 8:16, s * 128:(s + 1) * 128], psb(br).rearrange("p (hh i) -> p hh i", hh=8),
               gnwc[:].unsqueeze(2).broadcast_to([128, 8, 128]), ALU.mult)

        muT = actT[:, 0:8, :]

        def band_unit(T, s):
            first = (T == 0 and s == 0)
            b = None
            for g in range(4):
                if g % 2 == 0:
                    b = nb()
                for m in range(2):
                    c = 2 * g + m
                    o = ps[b][:, ((g % 2) * 2 + m) * 128:((g % 2) * 2 + m + 1) * 128]
                    mm(o, zu[:, s, c * 128:(c + 1) * 128], bands3[:, (8 + g) if first else g, :], True, first)
                    if not first:
                        prev = zu[:, s - 1, c * 128:(c + 1) * 128] if s > 0 else ulast[:, c * 128:(c + 1) * 128]
                        mm(o, prev, bands3[:, 4 + g, :], False, True)
                src = ps[b][:, (g % 2) * 256:(g % 2 + 1) * 256].rearrange("p (m t) -> p m t", m=2)
                dst = muT[:, 2 * g:2 * g + 2, s * 128:(s + 1) * 128]
                if first:
                    tt(dst, src, invcnt0[:, g * 128:(g + 1) * 128].unsqueeze(1).broadcast_to([128, 2, 128]), ALU.mult)
                elif g % 2 == 1:
                    act(muT[:, 2 * g - 2:2 * g + 2, s * 128:(s + 1) * 128],
                        ps[b][:].rearrange("p (m t) -> p m t", m=4), AF.Copy)

        def phase_mix_tail(T):
            slab0 = load_slab(w_in_d[:, 0:512], KC, 512)
            slab1 = load_slab(w_in_d[:, 512:1024], KC, 512)
            fillers = []
            for s_ in range(NSUB):
                fillers.append(("u", s_, lambda s_=s_: in_unit(0, s_, slab0)))
                fillers.append(("u", s_, lambda s_=s_: in_unit(1, s_, slab1)))
            for s_ in range(NSUB):
                fillers.append(("b", s_, lambda s_=s_: band_unit(T, s_)))
            pieces = [("A1", 0), ("A2", 0), ("Bpe", 0), ("Bgn", 0)]
            for s_ in range(1, NSUB):
                pieces += [("A1", s_), ("A2", s_), ("Bpe", s_), ("C", s_ - 1), ("Bgn", s_)]
            pieces.append(("C", NSUB - 1))
            fi = 0
            u_done = [0] * NSUB

            def run_filler():
                nonlocal fi
                if fi < len(fillers):
                    kind, s_, fn = fillers[fi]
                    if kind == "u":
                        u_done[s_] += 1
                    else:
                        assert all(u == 2 for u in u_done)
                    fn()
                    fi += 1

            for kind, s_ in pieces:
                if kind == "A1":
                    ret_A1(s_)
                elif kind == "A2":
                    ret_A2(s_)
                elif kind == "Bpe":
                    ret_Bpe(s_)
                elif kind == "Bgn":
                    ret_Bgn(s_)
                    continue
                else:
                    while u_done[s_] < 2:
                        run_filler()
                    ret_C(s_)
                run_filler()
            while fi < len(fillers):
                run_filler()
            act(ulast, zu[:, NSUB - 1, :], AF.Copy)
            for g in range(4):
                bb_ = [nb(), nb()]
                for m in range(2):
                    for kc in range(2):
                        mm(ps[bb_[m]][:], poolw4[:, g, kc, m * 128:(m + 1) * 128], muT[:, 2 * g + kc, :], kc == 0, kc == 1)
                for m in range(2):
                    oc = 2 * g + m
                    act(actT[:, oc, :], ps[bb_[m]][:], AF.Copy, scale=plsc[:, oc:oc + 1])

        def phase_out():
            for n in range(4):
                slab = load_slab(w_out_d[:, n * 512:(n + 1) * 512], KC, 512)
                for s in range(NSUB):
                    b = nb()
                    for kc in range(KC):
                        mm(ps[b][:], actT[:, kc, s * 128:(s + 1) * 128], slab[:, kc, :], kc == 0, kc == KC - 1)
                    hv = h[:, s, n * 512:(n + 1) * 512]
                    tt(hv, ps[b][:], hv, ALU.add)
                    if n == 3:
                        if s >= 2:
                            norm_tr(1, s - 2)
                        norm_elem(s)
            for s in range(NSUB - 2, NSUB):
                norm_tr(1, s)

        up_ctr = [0]

        def evac_conv(fi, bank, yb, acc):
            act(yb[:, 0:2], carry3[:, fi, :], AF.Copy)
            act(yb[:, 2:514], ps[bank][:], AF.Copy)
            act(carry3[:, fi, :], yb[:, 512:514], AF.Copy)
            act(acc, ps[bank][:], AF.Identity, scale=cw3[:, fi, 2:3], bias=cb[:, fi:fi + 1])
            stt_(acc, yb[:, 1:513], cw3[:, fi, 1:2], acc, ALU.mult, ALU.add)
            stt_(acc, yb[:, 0:512], cw3[:, fi, 0:1], acc, ALU.mult, ALU.add)

        def phase_up():
            for j in range(11):
                ncol = 512 if j < 10 else DFF - 10 * 512
                nch = ncol // 128
                sg_ = load_slab(w_up_d[:, j * 512:j * 512 + ncol], KC, ncol)
                for m in range(nch):
                    f = j * 4 + m
                    bg = nb()
                    for kc in range(KC):
                        mm(ps[bg][:], sg_[:, kc, m * 128:(m + 1) * 128], actT[:, kc, :], kc == 0, kc == KC - 1)
                    evac_conv(f, bg, ybuf[0][f % 2], accb[0][f % 2])
                    act(sgt[f % 5], accb[0][f % 2], AF.Silu)
                sv_ = load_slab(w_up_d[:, DFF + j * 512:DFF + j * 512 + ncol], KC, ncol)
                for m in range(nch):
                    f = j * 4 + m
                    bv = nb()
                    for kc in range(KC):
                        mm(ps[bv][:], sv_[:, kc, m * 128:(m + 1) * 128], actT[:, kc, :], kc == 0, kc == KC - 1)
                    evac_conv(f + NFC, bv, ybuf[1][f % 2], accb[1][f % 2])
                    tt(hidT[:, f, :], accb[1][f % 2], sgt[f % 5], ALU.mult)

        def phase_down(T):
            p_load(T)
            parts = [(0, 16), (16, 16), (32, 11)]
            for n in range(4):
                if n == 1:
                    p_tr()
                banks = [nb() for _ in range(NSUB)]
                for pi, (f0, nf) in enumerate(parts):
                    slab = load_slab(w_down_d[f0 * 128:(f0 + nf) * 128, n * 512:(n + 1) * 512], nf, 512)
                    for s in range(NSUB):
                        for fl in range(nf):
                            f = f0 + fl
                            mm(ps[banks[s]][:], hidT[:, f, s * 128:(s + 1) * 128], slab[:, fl, :], f == 0, f == NFC - 1)
                        if n == 3 and pi == 2:
                            hv = h[:, s, n * 512:(n + 1) * 512]
                            tt(hv, ps[banks[s]][:], hv, ALU.add)
                            if s >= 2:
                                norm_tr(2, s - 2)
                            norm_elem(s)
                if n < 3:
                    for s in range(NSUB):
                        hv = h[:, s, n * 512:(n + 1) * 512]
                        tt(hv, ps[banks[s]][:], hv, ALU.add)
            for s in range(NSUB - 2, NSUB):
                norm_tr(2, s)

        def p_load(T):
            t0 = T * TT
            pv = p_d[t0:t0 + TT, :].rearrange("(s p) d -> p s d", p=128)
            A("sp", lambda e: e.dma_start(out=p32, in_=pv), w=[p32], dkey="p")
            act(pbf, p32, AF.Copy)

        def p_tr():
            for s in range(NSUB):
                b = nb()
                for k2 in range(2):
                    tr(psb(b)[:, k2 * 128:(k2 + 1) * 128], pbf[:, s, k2 * 128:(k2 + 1) * 128])
                act(pT[:, :, s * 128:(s + 1) * 128], psb(b)[:, 0:256].rearrange("p (c t) -> p c t", c=2), AF.Copy)

        def phase_ple_prep(T):
            A("sp", lambda e: e.dma_start(out=wf_tab, in_=wf_d), w=[wf_tab], dkey="wf")

        ple_ctr = [0]

        def phase_ple(T):
            for n in range(4):
                sg_ = load_slab(wg_d[:, n * 512:(n + 1) * 512], KC, 512)
                if n == 0:
                    ppsrc = wp_d.rearrange("(c p) n -> p c n", p=128)
                    A("pool", lambda e: e.dma_start(out=ppw, in_=ppsrc), w=[ppw], dkey="pw")
                for s in range(NSUB):
                    k = ple_ctr[0] % 2
                    ple_ctr[0] += 1
                    ba = nb()
                    for kc in range(KC):
                        mm(ps[ba][:], actT[:, kc, s * 128:(s + 1) * 128], sg_[:, kc, :], kc == 0, kc == KC - 1)
                    bb = nb()
                    for k2 in range(2):
                        mm(ps[bb][:], pT[:, k2, s * 128:(s + 1) * 128], ppw[:, k2, n * 512:(n + 1) * 512], k2 == 0, k2 == 1)
                    act(sgm[k], ps[ba][:], AF.Sigmoid)
                    tt(tmp2[k], sgm[k], ps[bb][:], ALU.mult)
                    hv = h[:, s, n * 512:(n + 1) * 512]
                    tt(hv, hv, tmp2[k], ALU.add)
                    if n == 1 and T + 1 < nt:
                        n1_prefetch(T + 1, s)
                    if n == 3 and T + 1 == nt:
                        final_sub(T, s)
            if T + 1 < nt:
                load_rot(T + 1)
                for s in range(NSUB):
                    norm_tr(0, s, abfn[s])
                for s in range(NSUB):
                    final_sub(T, s)

        def final_sub(T, s):
            t0 = T * TT
            rstd_sub(s)
            og = ostage[s % 2]
            stt_(og, h[:, s, :], rsc(s), wf_tab, ALU.mult, ALU.mult)
            ov = out_d[t0 + s * 128:t0 + (s + 1) * 128, :]
            A("sp", lambda e, ov=ov, og=og: e.dma_start(out=ov, in_=og), r=[og], dkey=f"o{s % 2}")
            if T + 1 < nt:
                load_x(T + 1, s)

        def load_rot(T):
            A("sp", lambda e: e.dma_start(out=rot_flat, in_=rot_d[T]), w=[rot_flat], dkey="rot")

        def load_x(T, s):
            t0 = T * TT
            xv = x_d[t0 + s * 128:t0 + (s + 1) * 128, :]
            hv = h[:, s, :]
            A("sp", lambda e: e.dma_start(out=hv, in_=xv), w=[hv], dkey=f"x{s}")

        def store(T):
            pass

        phases = ["n1", "in", "mix", "out", "up", "down", "n3", "fin"]
        last = phases.index(stop_after) if stop_after else len(phases) - 1
        for T in range(nt):
            t0 = T * TT
            if T == 0:
                for s_ in range(NSUB):
                    load_x(0, s_)
            if T == 0:
                load_rot(0)
            steps = [(lambda: norm_to_T(0)) if T == 0 else (lambda: None), phase_in, lambda T=T: phase_mix_tail(T), phase_out,
                     phase_up, lambda T=T: phase_down(T),
                     lambda T=T: phase_ple_prep(T), lambda T=T: phase_ple(T)]
            for i, st in enumerate(steps):
                if i <= last:
                    st()
            store(T)
        P.analyze()

        @block.tensor
        def _(e):
            P.emit("pe", e, sems)

        @block.scalar
        def _(e):
            P.emit("act", e, sems)

        @block.vector
        def _(e):
            P.emit("dve", e, sems)

        @block.gpsimd
        def _(e):
            P.emit("pool", e, sems)

        @block.sync
        def _(e):
            P.emit("sp", e, sems)
            e.wait_ge(sems[("d", "o0")], 16 * P.dcount["o0"])
            e.wait_ge(sems[("d", "o1")], 16 * P.dcount["o1"])
    nc._prog_stats = {"n_ops": len(P.ops), "sig": P.sig_total}
    return nc


def make_in_maps(inputs, nt=4, cores=8):
    c = make_consts(nt)
    g = lambda k: np.asarray(inputs[k], np.float32)
    conv_w = g("conv_w")[0]
    shared = {
        "w_in": np.ascontiguousarray(g("w_in")[0]),
        "pool_w": np.ascontiguousarray(g("pool_w")[0].reshape(4 * 256, 256)),
        "w_out": np.ascontiguousarray(g("w_out")[0]),
        "w_up": np.ascontiguousarray(g("w_up")[0]),
        "w_down": np.ascontiguousarray(g("w_down")[0]),
        "ple_gate_w": np.ascontiguousarray(g("ple_gate_w")[0]),
        "ple_proj_w": np.ascontiguousarray(g("ple_proj_w")[0]),
        "nwc": np.ascontiguousarray(np.concatenate([cols(g("norm1_w")[0], 16), cols(g("norm2_w")[0], 16),
                                                    cols(g("norm3_w")[0], 16)], axis=1)),
        "plsc": cols(g("pool_scale")[0], 8),
        "gnwc": cols(g("ret_gn_w")[0], 8),
        "cw": np.ascontiguousarray(conv_w.reshape(3, 86, 128).transpose(2, 1, 0).reshape(128, 86 * 3)),
        "cb": cols(g("conv_b")[0], 86),
        "wf_tab": np.ascontiguousarray(np.broadcast_to(g("final_norm_w").reshape(1, D), (128, D))),
        "ident": c["ident"], "rot": c["rot"], "gtab": c["gtab"], "maskT": c["maskT"],
        "bands": c["bands"], "invcnt0": c["invcnt0"],
    }
    x = g("x")
    p = g("p")[0]
    ntok = nt * TT
    maps = []
    for b in range(cores):
        m = dict(shared)
        m["x"] = np.ascontiguousarray(x[b, :ntok])
        m["p"] = np.ascontiguousarray(p[b, :ntok])
        maps.append(m)
    return maps


_NC_CACHE = {}


def kernel(**inputs):
    if 4 not in _NC_CACHE:
        _NC_CACHE[4] = build_nc(4)
    nc = _NC_CACHE[4]
    maps = make_in_maps(inputs, 4, 8)
    res = run_bass_kernel_spmd(nc, maps, core_ids=list(range(8)))
    out = np.stack([np.asarray(r["out"], dtype=np.float32) for r in res.results], axis=0)
    return out
```
